# Optimizing a Trainium2 kernel written in Bass

```python
import math
import jax, jax.numpy as jnp
from jax import lax
import numpy as np

D_MODEL = 1024
BATCH = 8
SEQ = 2048
DEPTH = 4

HEAD_DIM = 64
A_GROUPS = ((128, 1), (512, 4), (2048, 16))
A_HEADS_PER_GROUP = 4
A_HEADS = A_HEADS_PER_GROUP * len(A_GROUPS)
A_WIDTH = A_HEADS * HEAD_DIM
B_HEADS = D_MODEL // HEAD_DIM
B_WIDTH = B_HEADS * HEAD_DIM
N_A = DEPTH // 2
N_B = DEPTH - N_A
D_FF = 2816
CONV_W = 3
ROPE_DIM = HEAD_DIM // 4
ROPE_THETA = 500000.0
BLK = 128
EPS = 1e-6
NEG = -1e30

kernel_name = "yoco_dilated_fox_convffn_trunk"


def rms_norm(x, g):
    x32 = x.astype(jnp.float32)
    y = x32 * lax.rsqrt(jnp.mean(x32 * x32, axis=-1, keepdims=True) + EPS)
    return (y * g.astype(jnp.float32)).astype(x.dtype)


def rope_tables(T):
    pos = jnp.arange(T, dtype=jnp.float32)
    inv = ROPE_THETA ** (-jnp.arange(0, ROPE_DIM, 2, dtype=jnp.float32) / ROPE_DIM)
    ang = pos[:, None] * inv[None, :]
    return jnp.cos(ang), jnp.sin(ang)


def apply_rope(t, cos, sin):
    half = ROPE_DIM // 2
    c = cos[None, :, None, :].astype(t.dtype)
    s = sin[None, :, None, :].astype(t.dtype)
    x1 = t[..., :half]
    x2 = t[..., half:ROPE_DIM]
    return jnp.concatenate([x1 * c - x2 * s, x2 * c + x1 * s, t[..., ROPE_DIM:]], axis=-1)


def banded_attention(q, k, v, n_back):
    N, L, H, D = q.shape
    nb = L // BLK
    qb = q.reshape(N, nb, BLK, H, D)
    kb = k.reshape(N, nb, BLK, H, D)
    vb = v.reshape(N, nb, BLK, H, D)

    def with_prev(t):
        prev = jnp.pad(t[:, :-1], ((0, 0), (1, 0), (0, 0), (0, 0), (0, 0)))
        return jnp.concatenate([prev, t], axis=2)

    kc, vc = with_prev(kb), with_prev(vb)
    s = jnp.einsum('nbqhd,nbkhd->nbhqk', qb, kc).astype(jnp.float32) * (D ** -0.5)
    rel = (jnp.arange(BLK)[:, None] + BLK) - jnp.arange(2 * BLK)[None, :]
    band = (rel >= 0) & (rel <= n_back)
    has_prev = (jnp.arange(nb)[:, None, None] > 0) | (jnp.arange(2 * BLK)[None, None, :] >= BLK)
    mask = band[None] & has_prev
    s = jnp.where(mask[None, :, None], s, NEG)
    m = jnp.max(s, axis=-1, keepdims=True)
    p = jnp.exp(s - m)
    l = jnp.sum(p, axis=-1, keepdims=True)
    o = jnp.einsum('nbhqk,nbkhd->nbqhd', (p / l).astype(v.dtype), vc)
    lse = (m + jnp.log(l))[..., 0]
    return o.reshape(N, L, H, D), lse.transpose(0, 1, 3, 2).reshape(N, L, H)


def dilated_mixer(xn, w_qkv, w_o, cos, sin):
    B, T, _ = xn.shape
    qkv = xn @ w_qkv
    q, k, v = jnp.split(qkv, 3, axis=-1)
    q = apply_rope(q.reshape(B, T, A_HEADS, HEAD_DIM), cos, sin)
    k = apply_rope(k.reshape(B, T, A_HEADS, HEAD_DIM), cos, sin)
    v = v.reshape(B, T, A_HEADS, HEAD_DIM)
    G = A_HEADS_PER_GROUP
    outs, lses = [], []
    for g, (window, r) in enumerate(A_GROUPS):
        L = T // r
        Lp = -(-L // BLK) * BLK

        def gather(t):
            t = t[:, :, g * G:(g + 1) * G].reshape(B, L, r, G, HEAD_DIM)
            t = t.transpose(0, 2, 1, 3, 4).reshape(B * r, L, G, HEAD_DIM)
            return jnp.pad(t, ((0, 0), (0, Lp - L), (0, 0), (0, 0)))

        o, lse = banded_attention(gather(q), gather(k), gather(v), window // r)
        o = o[:, :L].reshape(B, r, L, G, HEAD_DIM).transpose(0, 2, 1, 3, 4).reshape(B, T, G, HEAD_DIM)
        lse = lse[:, :L].reshape(B, r, L, G).transpose(0, 2, 1, 3).reshape(B, T, G)
        outs.append(o)
        lses.append(lse)
    alpha = jax.nn.softmax(jnp.stack(lses, axis=0), axis=0)
    o = jnp.concatenate([outs[g] * alpha[g][..., None].astype(outs[g].dtype)
                         for g in range(len(A_GROUPS))], axis=2)
    return o.reshape(B, T, A_WIDTH) @ w_o


def fox_mixer(xn, w_q, w_o, k, v, c):
    B, T, _ = xn.shape
    q = (xn @ w_q).reshape(B, T, B_HEADS, HEAD_DIM)
    c_t = c.transpose(0, 2, 1)
    scale = HEAD_DIM ** -0.5
    outs = []
    for i in range(T // BLK):
        q0, q1 = i * BLK, (i + 1) * BLK
        s = jnp.einsum('bqhd,bkhd->bhqk', q[:, q0:q1], k[:, :q1]).astype(jnp.float32) * scale
        s = s + (c_t[:, :, q0:q1, None] - c_t[:, :, None, :q1])
        causal = jnp.arange(q0, q1)[:, None] >= jnp.arange(q1)[None, :]
        s = jnp.where(causal, s, NEG)
        p = jax.nn.softmax(s, axis=-1).astype(v.dtype)
        outs.append(jnp.einsum('bhqk,bkhd->bqhd', p, v[:, :q1]))
    o = jnp.concatenate(outs, axis=1)
    return o.reshape(B, T, B_WIDTH) @ w_o


def conv_ffn(xn, w_up, cw, cb, w_down):
    a = xn @ w_up
    T = a.shape[1]
    ap = jnp.pad(a, ((0, 0), (CONV_W - 1, 0), (0, 0)))
    a = sum(ap[:, j:j + T] * cw[j] for j in range(CONV_W)) + cb
    gate, val = jnp.split(a, 2, axis=-1)
    return (jax.nn.gelu(gate, approximate=True) * val) @ w_down


def setup_inputs(seed: int = 0) -> dict:
    key = jax.random.key(seed)
    ks = jax.random.split(key, 14)

    def nrm(k, shape, fan_in):
        return jax.random.normal(k, shape, jnp.float32) * fan_in ** -0.5

    return {
        "x": jax.random.normal(ks[0], (BATCH, SEQ, D_MODEL), jnp.float32),
        "norm_gains": 1.0 + 0.05 * jax.random.normal(ks[1], (DEPTH, 4, D_MODEL), jnp.float32),
        "w_qkv_a": nrm(ks[2], (N_A, D_MODEL, 3 * A_WIDTH), D_MODEL),
        "w_o_a": nrm(ks[3], (N_A, A_WIDTH, D_MODEL), A_WIDTH),
        "w_q_b": nrm(ks[4], (N_B, D_MODEL, B_WIDTH), D_MODEL),
        "w_o_b": nrm(ks[5], (N_B, B_WIDTH, D_MODEL), B_WIDTH),
        "kv_norm": 1.0 + 0.05 * jax.random.normal(ks[6], (D_MODEL,), jnp.float32),
        "w_kvf": nrm(ks[7], (D_MODEL, 2 * B_WIDTH + B_HEADS), D_MODEL),
        "b_f": 3.0 + 0.5 * jax.random.normal(ks[8], (B_HEADS,), jnp.float32),
        "w_up": nrm(ks[9], (DEPTH, D_MODEL, 2 * D_FF), D_MODEL),
        "conv_w": nrm(ks[10], (DEPTH, CONV_W, 2 * D_FF), CONV_W),
        "conv_b": 0.01 * jax.random.normal(ks[11], (DEPTH, 2 * D_FF), jnp.float32),
        "w_down": nrm(ks[12], (DEPTH, D_FF, D_MODEL), D_FF),
    }


def reference(x, norm_gains, w_qkv_a, w_o_a, w_q_b, w_o_b, kv_norm, w_kvf, b_f,
              w_up, conv_w, conv_b, w_down):
    B, T, _ = x.shape
    cos, sin = rope_tables(T)
    h = x
    k_sh = v_sh = c_sh = None
    for l in range(DEPTH):
        g = norm_gains[l]
        if l < N_A:
            mix = dilated_mixer(rms_norm(h, g[0]), w_qkv_a[l], w_o_a[l], cos, sin)
        else:
            if l == N_A:
                kvf = rms_norm(h, kv_norm) @ w_kvf
                k_sh = kvf[..., :B_WIDTH].reshape(B, T, B_HEADS, HEAD_DIM)
                v_sh = kvf[..., B_WIDTH:2 * B_WIDTH].reshape(B, T, B_HEADS, HEAD_DIM)
                log_f = jax.nn.log_sigmoid((kvf[..., 2 * B_WIDTH:] + b_f).astype(jnp.float32))
                c_sh = jnp.cumsum(log_f, axis=1)
            j = l - N_A
            mix = fox_mixer(rms_norm(h, g[0]), w_q_b[j], w_o_b[j], k_sh, v_sh, c_sh)
        h = h + rms_norm(mix, g[1])
        f = conv_ffn(rms_norm(h, g[2]), w_up[l], conv_w[l], conv_b[l], w_down[l])
        h = h + rms_norm(f, g[3])
    return h
```

```python
import numpy as np
from contextlib import ExitStack
import concourse.bass as bass
import concourse.mybir as mybir
from concourse.bass_utils import run_bass_kernel_spmd

F32 = mybir.dt.float32
BF16 = mybir.dt.bfloat16
AF = mybir.ActivationFunctionType
ALU = mybir.AluOpType

D = 1024
T = 2048
DEPTH = 4
NA = 2
DFF = 2816
NFC = 22
NTB = 4
EPS = 1e-6
ENGS = ("sync", "scalar", "gpsimd", "vector", "tensor")
import os as _os
SAME_ENGINE_WAITS = _os.environ.get("SEW", "0") == "1"


class TB:
    def __init__(self, ap=None):
        self.ap = ap
        self.w = {}
        self.r = {}


def _merge(d, tk):
    s, v = tk
    k = id(s)
    if k not in d or d[k][1] < v:
        d[k] = (s, v)


class Sched:
    def __init__(self, nc, es):
        self.nc = nc
        self.es = es
        self.ops = {e: [] for e in ENGS}
        self.esem = {}
        self.ecnt = {}
        for e in ("scalar", "gpsimd", "vector", "tensor"):
            self.esem[e] = es.enter_context(nc.semaphore("s_" + e))
            self.ecnt[e] = 0
        self.dsem = {}
        self.dcnt = {}
        self.seen = {e: {} for e in ENGS}
        self.bar = {e: [] for e in ENGS}
        self.nops = 0

    def _deps(self, eng, reads, writes, waits):
        d = {}
        for b in reads:
            for tk in b.w.values():
                _merge(d, tk)
        for b in writes:
            for tk in b.w.values():
                _merge(d, tk)
            for tk in b.r.values():
                _merge(d, tk)
        for tk in waits:
            if tk is not None:
                _merge(d, tk)
        for tk in self.bar[eng]:
            _merge(d, tk)
        self.bar[eng] = []
        out = []
        for (s, v) in d.values():
            if eng in self.esem and s is self.esem[eng]:
                if eng == "tensor" or not SAME_ENGINE_WAITS:
                    continue
            out.append((s, v))
        return out

    def _commit(self, tk, reads, writes):
        for b in writes:
            b.w = {}
            b.r = {}
            _merge(b.w, tk)
        for b in reads:
            _merge(b.r, tk)

    def op(self, eng, fn, reads=(), writes=(), waits=()):
        return self.group(eng, [fn], reads, writes, waits)

    def group(self, eng, fns, reads=(), writes=(), waits=()):
        deps = self._deps(eng, reads, writes, waits)
        self.ecnt[eng] += 1
        tk = (self.esem[eng], self.ecnt[eng])
        n = len(fns)
        for i, fn in enumerate(fns):
            self.ops[eng].append((fn, tuple(deps) if i == 0 else (), (self.esem[eng], 1) if i == n - 1 else None))
        self.nops += n
        self._commit(tk, reads, writes)
        return tk

    def dma(self, key, fn, reads=(), writes=(), waits=(), eng="sync"):
        deps = self._deps(eng, reads, writes, waits)
        if key not in self.dsem:
            self.dsem[key] = self.es.enter_context(self.nc.semaphore("d_" + key))
            self.dcnt[key] = 0
        self.dcnt[key] += 16
        tk = (self.dsem[key], self.dcnt[key])
        self.ops[eng].append((fn, tuple(deps), (self.dsem[key], 16)))
        self.nops += 1
        self._commit(tk, reads, writes)
        return tk

    def all_tickets(self):
        tks = [(self.esem[e], self.ecnt[e]) for e in self.esem if self.ecnt[e] > 0]
        tks += [(self.dsem[k], self.dcnt[k]) for k in self.dsem]
        return tks

    def barrier(self):
        tks = self.all_tickets()
        for e in ENGS:
            self.bar[e] = list(tks)

    def flush(self, final=False):
        nc = self.nc
        fin = self.all_tickets() if final else []
        with nc.Block() as block:
            def mk(ename):
                def body(engine):
                    seen = self.seen[ename]
                    for fn, waits, inc in self.ops[ename]:
                        for (s, v) in waits:
                            k = id(s)
                            if seen.get(k, 0) >= v:
                                continue
                            seen[k] = v
                            engine.wait_ge(s, v)
                        ins = fn(engine)
                        if inc is not None:
                            ins.then_inc(inc[0], inc[1])
                    if ename == "sync":
                        for (s, v) in fin:
                            engine.wait_ge(s, v)
                return body
            block.sync(mk("sync"))
            block.scalar(mk("scalar"))
            block.gpsimd(mk("gpsimd"))
            block.vector(mk("vector"))
            block.tensor(mk("tensor"))
        self.ops = {e: [] for e in ENGS}


class Ring:
    def __init__(self, aps):
        self.bufs = [TB(a) for a in aps]
        self.i = -1

    def next(self):
        self.i = (self.i + 1) % len(self.bufs)
        return self.bufs[self.i]


def build(stop=None, dbg=None):
    nc = bass.Bass("TRN2", target_bir_lowering=False)

    def din(name, shape, dt=F32):
        return nc.dram_tensor(name, list(shape), dt, kind="ExternalInput").ap()

    xT = din("xT", [D, T])
    gcols_d = din("gcols", [128, 136])
    wqkv_d = din("wqkv", [NA, 6, 5, 128, 8, 128])
    woa_d = din("woa", [NA, 128, 6, 1024])
    wqb_d = din("wqb", [2, 8, 128, 8, 128])
    wob_d = din("wob", [2, 128, 8, 1024])
    wk_d = din("wk", [8, 128, 8, 128])
    wv_d = din("wv", [8, 128, 8, 128])
    wf_d = din("wf", [128, 8, 16])
    bfneg_d = din("bfneg", [16, 1])
    wup_d = din("wup", [DEPTH, NFC, 128, 8, 256])
    cp_d = din("cp", [128, DEPTH * 44 * 4])
    wdn_d = din("wdn", [DEPTH, 8, 128, NFC, 128])
    ropeC_d = din("ropeC", [128, T])
    ropeS_d = din("ropeS", [128, T])
    maskA_d = din("maskA", [128, 256])
    ident_d = din("ident", [16, 16])
    ident128_d = din("ident128", [128, 128])
    permR_d = din("permR", [128, 128])
    outT = nc.dram_tensor("outT", [D, T], F32, kind="ExternalOutput").ap()

    hA = nc.dram_tensor("hA", [D, T], F32).ap()
    hB = nc.dram_tensor("hB", [D, T], F32).ap()
    Kd = nc.dram_tensor("Kd", [16, 64, T], BF16).ap()
    Vd = nc.dram_tensor("Vd", [16, 128, 16, 64], BF16).ap()
    Cq = nc.dram_tensor("Cq", [16, 3, T], BF16).ap()

    def hview(h):
        return h.rearrange("(c p) t -> p c t", p=128)

    with ExitStack() as es:
        S = Sched(nc, es)

        uniq = [0]
        dumps = {}

        def dump(name, ap, shape, dt, reads):
            if dbg is None or name not in dbg or name in dumps:
                return
            d = nc.dram_tensor("dbg_" + name, list(shape), dt, kind="ExternalOutput").ap()
            dumps[name] = d
            S.dma("dbg_" + name, lambda e: e.dma_start(out=d, in_=ap), reads=reads, eng="gpsimd")

        def sb(scope, name, shape, dt):
            uniq[0] += 1
            return scope.enter_context(nc.sbuf_tensor("sb%d_%s" % (uniq[0], name), list(shape), dt))

        banks = [es.enter_context(nc.psum_tensor("bank%d" % i, [128, 512], F32)) for i in range(8)]
        psb = [TB(b) for b in banks]

        gcols = sb(es, "gcols", [128, 136], F32)
        cp = sb(es, "cp", [128, DEPTH * 44 * 4], F32)
        ones_bf = sb(es, "ones_bf", [128, 128], BF16)
        mask_f = sb(es, "mask_f", [128, 256], F32)
        maskA = sb(es, "maskA", [128, 256], BF16)
        epsc = sb(es, "epsc", [128, 1], F32)
        onec = sb(es, "onec", [128, 1], F32)
        ident = sb(es, "ident", [16, 16], F32)
        bfneg = sb(es, "bfneg", [16, 1], F32)
        ckneg = sb(es, "ckneg", [128, 256], F32)
        id_f = sb(es, "id_f", [128, 128], F32)
        identb = sb(es, "identb", [128, 128], BF16)
        maskB = sb(es, "maskB", [128, 256], BF16)
        permb = sb(es, "permb", [128, 128], BF16)
        CONST = TB()
        CKNEG = TB()

        S.dma("c0", lambda e: e.dma_start(out=gcols[:, :], in_=gcols_d[:, :]), writes=[CONST])
        S.dma("c0", lambda e: e.dma_start(out=cp[:, :], in_=cp_d[:, :]), writes=[CONST])
        S.dma("c0", lambda e: e.dma_start(out=mask_f[:, :], in_=maskA_d[:, :]), writes=[CONST])
        S.dma("c0", lambda e: e.dma_start(out=ident[:, :], in_=ident_d[:, :]), writes=[CONST])
        S.dma("c0", lambda e: e.dma_start(out=bfneg[:, :], in_=bfneg_d[:, :]), writes=[CONST])
        tc0 = (S.dsem["c0"], S.dcnt["c0"])
        S.op("gpsimd", lambda e: e.memset(ones_bf[:, :], 1.0), writes=[CONST])
        S.op("gpsimd", lambda e: e.memset(epsc[:, :], EPS), writes=[CONST])
        S.op("gpsimd", lambda e: e.memset(onec[:, :], 1.0), writes=[CONST])
        S.op("gpsimd", lambda e: e.tensor_copy(maskA[:, :], mask_f[:, :]), writes=[CONST], waits=[tc0])
        S.dma("c1", lambda e: e.dma_start(out=id_f[:, :], in_=ident128_d[:, :]), writes=[CONST])
        tc1 = (S.dsem["c1"], S.dcnt["c1"])
        S.op("gpsimd", lambda e: e.tensor_copy(identb[:, :], id_f[:, :]), writes=[CONST], waits=[tc1])
        S.dma("c2", lambda e: e.dma_start(out=id_f[:, :], in_=permR_d[:, :]), reads=[CONST], writes=[CONST])
        S.op("gpsimd", lambda e: e.tensor_copy(permb[:, :], id_f[:, :]), reads=[CONST], writes=[CONST])
        S.op("gpsimd", lambda e: e.tensor_scalar(maskB[:, :], mask_f[:, :], -1.0, 30000.0, ALU.add, ALU.mult), writes=[CONST], waits=[tc0])
        S.barrier()

        def gcol(l, i, c):
            k = (l * 4 + i) * 8 + c if l < DEPTH else 128 + c
            return gcols[:, k:k + 1]

        def rstd_from(src_tb, src3, sq_ring, rs_ring, ps_ss):
            sq = sq_ring.next()
            S.op("scalar", lambda e: e.activation(sq.ap[:, :, :], src3, AF.Square), reads=[src_tb], writes=[sq])
            S.group("tensor", [(lambda c: lambda e: e.matmul(ps_ss.ap[:, :], ones_bf[:, :], sq.ap[:, c, :], start=(c == 0), stop=(c == 7)))(c) for c in range(8)],
                    reads=[sq], writes=[ps_ss])
            rs = rs_ring.next()
            S.op("scalar", lambda e: e.activation(rs.ap[:, :], ps_ss.ap[:, :], AF.Ln, bias=epsc[:, 0:1], scale=1.0 / D), reads=[ps_ss], writes=[rs])
            S.op("scalar", lambda e: e.activation(rs.ap[:, :], rs.ap[:, :], AF.Exp, scale=-0.5), reads=[rs], writes=[rs])
            return rs

        def norm_phase(scope, hsrc, l, gi, xn_tbs, xnT):
            hring = Ring([sb(scope, "n_h%d" % i, [128, 8, 512], F32) for i in range(3)])
            sq_ring = Ring([sb(scope, "n_sq%d" % i, [128, 8, 512], BF16) for i in range(2)])
            rs_ring = Ring([sb(scope, "n_rs%d" % i, [128, 512], F32) for i in range(2)])
            ssr = Ring([None] * 2)
            ssr.bufs = [psb[6], psb[7]]
            hv = hview(hsrc)
            st = {}

            def s1(tb):
                hb = hring.next()
                S.dma("nh%d" % hring.i, lambda e: e.dma_start(out=hb.ap[:, :, :], in_=hv[:, :, tb * 512:(tb + 1) * 512]), writes=[hb])
                sq = sq_ring.next()
                S.op("scalar", lambda e: e.activation(sq.ap[:, :, :], hb.ap[:, :, :], AF.Square), reads=[hb], writes=[sq])
                ps = ssr.next()
                S.group("tensor", [(lambda c: lambda e: e.matmul(ps.ap[:, :], ones_bf[:, :], sq.ap[:, c, :], start=(c == 0), stop=(c == 7)))(c) for c in range(8)],
                        reads=[sq], writes=[ps])
                st[tb] = (hb, ps)

            def s2(tb):
                hb, ps = st.pop(tb)
                rs = rs_ring.next()
                S.op("scalar", lambda e: e.activation(rs.ap[:, :], ps.ap[:, :], AF.Ln, bias=epsc[:, 0:1], scale=1.0 / D), reads=[ps], writes=[rs])
                S.op("scalar", lambda e: e.activation(rs.ap[:, :], rs.ap[:, :], AF.Exp, scale=-0.5), reads=[rs], writes=[rs])
                S.group("vector", [(lambda c: lambda e: e.scalar_tensor_tensor(
                    xnT[:, c, tb * 512:(tb + 1) * 512], hb.ap[:, c, :], gcol(l, gi, c), rs.ap[:, :], ALU.mult, ALU.mult))(c) for c in range(8)],
                    reads=[hb, rs], writes=[xn_tbs[tb]])
            s1(0); s1(1); s2(0); s1(2); s2(1); s1(3); s2(2); s2(3)

        class Resid:
            def __init__(self, scope, name, l, gi, hsrc, hdst, depth=2, depth_sq=None, nxt=None):
                depth_sq = depth if depth_sq is None else depth_sq
                self.hring = Ring([sb(scope, "%s_h%d" % (name, i), [128, 8, 512], F32) for i in range(depth)])
                self.sq_ring = Ring([sb(scope, "%s_sq%d" % (name, i), [128, 8, 512], BF16) for i in range(depth_sq)])
                self.rs_ring = Ring([sb(scope, "%s_rs%d" % (name, i), [128, 512], F32) for i in range(2)])
                self.hv = hview(hsrc)
                self.ov = hview(hdst)
                self.l, self.gi = l, gi
                self.nxt = nxt
                self.st_ = {}

            def step(self, tb, y3, ytb):
                self.step_a(tb, y3, ytb)
                self.step_b(tb)

            def step_a(self, tb, y3, ytb):
                l, gi = self.l, self.gi
                hv, ov = self.hv, self.ov
                hb = self.hring.next()
                self.st_[tb] = (hb, self.hring.i)
                S.dma("rh%d" % self.hring.i, lambda e: e.dma_start(out=hb.ap[:, :, :], in_=hv[:, :, tb * 512:(tb + 1) * 512]), writes=[hb])
                rs = rstd_from(ytb, y3, self.sq_ring, self.rs_ring, psb[7])
                S.group("vector", [(lambda c: lambda e: e.scalar_tensor_tensor(
                    y3[:, c, :], y3[:, c, :], gcol(l, gi, c), rs.ap[:, :], ALU.mult, ALU.mult))(c) for c in range(8)],
                    reads=[rs], writes=[ytb])
                S.op("vector", lambda e: e.tensor_tensor(hb.ap[:, :, :], hb.ap[:, :, :], y3, ALU.add), reads=[ytb], writes=[hb])

            def step_b(self, tb):
                ov = self.ov
                hb, hi_ = self.st_.pop(tb)
                if self.nxt is not None:
                    nl, ngi, nxn, nxn_tbs = self.nxt
                    rs2 = rstd_from(hb, hb.ap[:, :, :], self.sq_ring, self.rs_ring, psb[7])
                    S.group("vector", [(lambda c: lambda e: e.scalar_tensor_tensor(
                        nxn[:, c, tb * 512:(tb + 1) * 512], hb.ap[:, c, :], gcol(nl, ngi, c), rs2.ap[:, :], ALU.mult, ALU.mult))(c) for c in range(8)],
                        reads=[hb, rs2], writes=[nxn_tbs[tb]])
                S.dma("ro%d" % hi_, lambda e: e.dma_start(out=ov[:, :, tb * 512:(tb + 1) * 512], in_=hb.ap[:, :, :]), reads=[hb], eng="gpsimd")

        class WLoader:
            def __init__(self, scope, name, shape, nst=2, nbf=2, eng="gpsimd"):
                self.name = name
                self.eng = eng
                self.n = 0
                self.st = Ring([sb(scope, "%s_s%d" % (name, i), [128] + list(shape), F32) for i in range(nst)])
                self.bf = Ring([sb(scope, "%s_b%d" % (name, i), [128] + list(shape), BF16) for i in range(nbf)])

            def load(self, dram_ap):
                st = self.st.next()
                S.dma("%s%d" % (self.name, self.st.i), lambda e: e.dma_start(out=st.ap[:], in_=dram_ap), writes=[st])
                wb = self.bf.next()
                eng = self.eng
                if isinstance(eng, (list, tuple)):
                    eng = eng[self.n % len(eng)]
                    self.n += 1
                if eng == "scalar":
                    S.op("scalar", lambda e: e.copy(wb.ap[:], st.ap[:]), reads=[st], writes=[wb])
                else:
                    S.op(eng, lambda e: e.tensor_copy(wb.ap[:], st.ap[:]), reads=[st], writes=[wb])
                return wb

        def ffn(l, hsrc, hdst, xn_pre):
            with ExitStack() as us:
                uT = sb(us, "f_u", [128, NFC, T], BF16)
                u_tb = TB()
                with ExitStack() as fs:
                    xnT, xn_tbs = xn_pre
                    with ExitStack() as ps_:
                        wl = WLoader(ps_, "wu", [8, 256], 2, 2, eng="scalar")
                        abuf = Ring([sb(ps_, "f_a%d" % i, [128, 2, 514], F32) for i in range(3)])
                        acc = Ring([sb(ps_, "f_c%d" % i, [128, 2, 512], F32) for i in range(4)])
                        pring = Ring([None] * 6)
                        pring.bufs = [psb[i] for i in range(6)]

                        def cpc(ch, k):
                            o = ((l * 44 + ch) * 4) + k
                            return cp[:, o:o + 1]
                        iters = [(fc, tb) for fc in range(NFC) for tb in range(NTB)]
                        st = {}
                        wst_ = {"next": wl.load(wup_d[l, 0]), "cur": None}

                        def stage_a(i):
                            fc, tb = iters[i]
                            if tb == 0:
                                wst_["cur"] = wst_["next"]
                                if fc + 1 < NFC:
                                    wst_["next"] = wl.load(wup_d[l, fc + 1])
                            w = wst_["cur"]
                            pg = pring.next()
                            pv = pring.next()
                            tsl = slice(tb * 512, (tb + 1) * 512)
                            S.group("tensor", [(lambda k, pg, w, tsl: lambda e: e.matmul(pg.ap[:, :], w.ap[:, k, 0:128], xnT[:, k, tsl], start=(k == 0), stop=(k == 7)))(k, pg, w, tsl) for k in range(8)],
                                    reads=[w, xn_tbs[tb]], writes=[pg])
                            S.group("tensor", [(lambda k, pv, w, tsl: lambda e: e.matmul(pv.ap[:, :], w.ap[:, k, 128:256], xnT[:, k, tsl], start=(k == 0), stop=(k == 7)))(k, pv, w, tsl) for k in range(8)],
                                    reads=[w, xn_tbs[tb]], writes=[pv])
                            a = abuf.next()
                            c = acc.next()
                            if tb == 0:
                                S.op("gpsimd", (lambda a: lambda e: e.memset(a.ap[:, :, 0:2], 0.0))(a), writes=[a])
                            else:
                                pa = st[i - 1][0]
                                S.op("gpsimd", (lambda a, pa: lambda e: e.tensor_copy(a.ap[:, :, 0:2], pa.ap[:, :, 512:514]))(a, pa), reads=[pa], writes=[a])
                            S.op("scalar", (lambda a, pg: lambda e: e.copy(a.ap[:, 0, 2:514], pg.ap[:, :]))(a, pg), reads=[pg], writes=[a])
                            S.op("scalar", (lambda a, pv: lambda e: e.copy(a.ap[:, 1, 2:514], pv.ap[:, :]))(a, pv), reads=[pv], writes=[a])
                            S.op("scalar", (lambda c, pg, fc: lambda e: e.activation(c.ap[:, 0, :], pg.ap[:, :], AF.Identity, bias=cpc(fc, 3), scale=cpc(fc, 2)))(c, pg, fc), reads=[pg, CONST], writes=[c])
                            S.op("scalar", (lambda c, pv, fc: lambda e: e.activation(c.ap[:, 1, :], pv.ap[:, :], AF.Identity, bias=cpc(22 + fc, 3), scale=cpc(22 + fc, 2)))(c, pv, fc), reads=[pv, CONST], writes=[c])
                            st[i] = (a, c)

                        def stage_b(i):
                            fc, tb = iters[i]
                            a, c = st[i]
                            fns = []
                            for half, ch in ((0, fc), (1, 22 + fc)):
                                for (off, k) in ((1, 1), (0, 0)):
                                    fns.append((lambda c, a, half, ch, off, k: lambda e: e.scalar_tensor_tensor(
                                        c.ap[:, half, :], a.ap[:, half, off:off + 512], cpc(ch, k), c.ap[:, half, :], ALU.mult, ALU.add))(c, a, half, ch, off, k))
                            S.group("vector", fns, reads=[a], writes=[c])

                        def stage_c(i):
                            fc, tb = iters[i]
                            a, c = st[i]
                            tsl = slice(tb * 512, (tb + 1) * 512)
                            S.op("scalar", (lambda c: lambda e: e.activation(c.ap[:, 0, :], c.ap[:, 0, :], AF.Gelu_apprx_tanh))(c), reads=[c], writes=[c])
                            S.op("vector", (lambda c, fc, tsl: lambda e: e.tensor_tensor(uT[:, fc, tsl], c.ap[:, 0, :], c.ap[:, 1, :], ALU.mult))(c, fc, tsl), reads=[c], writes=[u_tb])

                        ni = len(iters)
                        for i in range(ni + 2):
                            if i < ni:
                                stage_a(i)
                            if 0 <= i - 1 < ni:
                                stage_b(i - 1)
                            if 0 <= i - 2 < ni:
                                stage_c(i - 2)
                        S.flush()
                    S.barrier()
                with ExitStack() as ds:
                    yh = xn_pre[0].bitcast(F32)
                    y_tbs = [TB(), TB()]
                    wl = WLoader(ds, "wd", [11, 128], 2, 4, eng=("scalar", "vector"))
                    rsd = Resid(ds, "fr", l, 3, hsrc, hdst, depth=2, depth_sq=1)
                    pring = Ring([None] * 6)
                    pring.bufs = [psb[i] for i in range(6)]
                    NPRE = 6
                    glist = [(th, oc, t2) for th in range(2) for oc in range(8) for t2 in range(2)]
                    wcur = {}

                    def wload(oc):
                        return [wl.load(wdn_d[l, oc, :, 0:11, :]), wl.load(wdn_d[l, oc, :, 11:22, :])]

                    def emit_mm(gi):
                        th, oc, t2 = glist[gi]
                        if t2 == 0:
                            if oc == 0:
                                wcur["w"] = wload(0)
                            else:
                                wcur["w"] = wcur["n"]
                            if oc + 1 < 8:
                                wcur["n"] = wload(oc + 1)
                        w2 = wcur["w"]
                        tb = th * 2 + t2
                        p = pring.next()
                        tsl = slice(tb * 512, (tb + 1) * 512)
                        S.group("tensor", [(lambda k: lambda e: e.matmul(p.ap[:, :], w2[k // 11].ap[:, k % 11, :], uT[:, k, tsl], start=(k == 0), stop=(k == NFC - 1)))(k) for k in range(NFC)],
                                reads=[w2[0], w2[1], u_tb], writes=[p])
                        return p

                    def emit_copy(gi, p):
                        th, oc, t2 = glist[gi]
                        if oc % 2 == 0:
                            S.op("scalar", lambda e: e.copy(yh[:, oc, t2 * 512:(t2 + 1) * 512], p.ap[:, :]), reads=[p], writes=[y_tbs[t2]])
                        else:
                            S.op("vector", lambda e: e.tensor_copy(yh[:, oc, t2 * 512:(t2 + 1) * 512], p.ap[:, :]), reads=[p], writes=[y_tbs[t2]])

                    for gi in range(16):
                        emit_copy(gi, emit_mm(gi))
                    pend = [(gi, emit_mm(gi)) for gi in range(16, 16 + NPRE)]
                    for t2 in range(2):
                        rsd.step(t2, yh[:, :, t2 * 512:(t2 + 1) * 512], y_tbs[t2])
                    for (gi, p) in pend:
                        emit_copy(gi, p)
                    for gi in range(16 + NPRE, 32):
                        emit_copy(gi, emit_mm(gi))
                    for t2 in range(2):
                        rsd.step(2 + t2, yh[:, :, t2 * 512:(t2 + 1) * 512], y_tbs[t2])
                    S.flush()
                S.barrier()

        def layer_a(l, hsrc, hdst, xn_next):
            with ExitStack() as fs:
                oT = sb(fs, "a_o", [128, 6, T], BF16)
                LS = sb(fs, "a_ls", [128, 2, T], F32)
                wo = sb(fs, "a_wo", [128, 6, 1024], BF16)
                WO = TB()
                o_tb = TB()
                ls_tb = TB()
                with ExitStack() as as_:
                    xnT = xn_next[0]
                    xn_tbs = [TB() for _ in range(NTB)]
                    with ExitStack() as ns:
                        norm_phase(ns, hsrc, l, 0, xn_tbs, xnT)
                        S.flush()
                    S.barrier()
                    wsr = Ring([sb(as_, "a_wos%d" % i, [128, 1024], F32) for i in range(3)])
                    for kc in range(6):
                        wst = wsr.next()
                        S.dma("wo%d" % wsr.i, (lambda kc, wst: lambda e: e.dma_start(out=wst.ap[:, :], in_=woa_d[l, :, kc, :]))(kc, wst), writes=[wst])
                        ce = ("gpsimd", "vector", "scalar")[kc % 3]
                        if ce == "scalar":
                            S.op("scalar", (lambda kc, wst: lambda e: e.copy(wo[:, kc, :], wst.ap[:, :]))(kc, wst), reads=[wst], writes=[WO])
                        else:
                            S.op(ce, (lambda kc, wst: lambda e: e.tensor_copy(wo[:, kc, :], wst.ap[:, :]))(kc, wst), reads=[wst], writes=[WO])
                    wl = WLoader(as_, "wa", [8, 128], 2, 5, eng=("scalar", "vector"))
                    ropeC = sb(as_, "a_rc", [128, T], F32)
                    ropeS = sb(as_, "a_rs", [128, T], F32)
                    ROPE = TB()
                    S.dma("rope", lambda e: e.dma_start(out=ropeC[:, :], in_=ropeC_d[:, :]), writes=[ROPE])
                    S.dma("rope", lambda e: e.dma_start(out=ropeS[:, :], in_=ropeS_d[:, :]), writes=[ROPE])
                    qk = [[TB(sb(as_, "a_%s%d" % (n, i), [128, T], BF16)) for n in ("q", "k")] for i in range(2)]
                    Vu = [TB(sb(as_, "a_v%d" % i, [128, 16, 2, 128], BF16)) for i in range(2)]
                    for i in range(2):
                        S.op("gpsimd", (lambda i: lambda e: e.memset(Vu[i].ap[:, :, :, :], 1.0))(i), writes=[Vu[i]])
                    t1r = Ring([sb(as_, "a_t1%d" % i, [128, 512], F32) for i in range(2)])
                    t2r = Ring([sb(as_, "a_t2%d" % i, [128, 512], F32) for i in range(2)])
                    qtr = Ring([sb(as_, "a_qt%d" % i, [128, 512], BF16) for i in range(2)])
                    ptr = Ring([sb(as_, "a_pt%d" % i, [128, 256], BF16) for i in range(6)])
                    pring = Ring([None] * 3)
                    pring.bufs = [psb[i] for i in range(3)]
                    sring = Ring([None] * 3)
                    sring.bufs = [psb[3], psb[4], psb[5]]
                    oring = Ring([None] * 2)
                    oring.bufs = [psb[6], psb[7]]
                    units = [(g, jj) for g in range(3) for jj in range(2)]
                    DIL = [1, 4, 16]
                    NB = [16, 4, 1]
                    LAG = 3

                    def load_unit(u):
                        return [wl.load(wqkv_d[l, u, pc]) for pc in (0, 2, 4)]

                    def proj(ui, ws):
                        g, jj = units[ui]
                        r = DIL[g]
                        qb, kb = qk[ui % 2]
                        vb = Vu[ui % 2]
                        for (wi, dst) in ((0, qb), (1, kb)):
                            for tb in range(NTB):
                                pa = pring.next()
                                pb = pring.next()
                                tsl = slice(tb * 512, (tb + 1) * 512)
                                S.group("tensor", [(lambda k, pa, tsl, wi, ws: lambda e: e.matmul(pa.ap[:, :], ws[wi].ap[:, k, :], xnT[:, k, tsl], start=(k == 0), stop=(k == 7)))(k, pa, tsl, wi, ws) for k in range(8)],
                                        reads=[ws[wi], xn_tbs[tb]], writes=[pa])
                                qt = qtr.next()
                                S.op("scalar", (lambda qt, pa: lambda e: e.copy(qt.ap[:, :], pa.ap[:, :]))(qt, pa), reads=[pa], writes=[qt])
                                S.op("tensor", (lambda pb, qt: lambda e: e.matmul(pb.ap[:, :], permb[:, :], qt.ap[:, :], start=True, stop=True))(pb, qt), reads=[qt, CONST], writes=[pb])
                                t1 = t1r.next()
                                t2 = t2r.next()
                                S.op("vector", (lambda t1, pa, tsl: lambda e: e.tensor_tensor(t1.ap[:, :], pa.ap[:, :], ropeC[:, tsl], ALU.mult))(t1, pa, tsl), reads=[pa, ROPE, qt], writes=[t1])
                                S.op("vector", (lambda t2, pb, tsl: lambda e: e.tensor_tensor(t2.ap[:, :], pb.ap[:, :], ropeS[:, tsl], ALU.mult))(t2, pb, tsl), reads=[pb, ROPE], writes=[t2])
                                if r == 1:
                                    oap = dst.ap[:, tsl]
                                    i1 = t1.ap[:, :]
                                    i2 = t2.ap[:, :]
                                else:
                                    m = 512 // r
                                    oap = dst.ap[:, :].rearrange("p (r l) -> p r l", r=r)[:, :, tb * m:(tb + 1) * m]
                                    i1 = t1.ap[:, :].rearrange("p (m r) -> p r m", r=r)
                                    i2 = t2.ap[:, :].rearrange("p (m r) -> p r m", r=r)
                                S.op("gpsimd", (lambda oap, i1, i2: lambda e: e.tensor_tensor(oap, i1, i2, ALU.add))(oap, i1, i2), reads=[t1, t2], writes=[dst])
                                yield
                        for m4 in range(4):
                            pa = pring.next()
                            fns = []
                            for a4 in range(4):
                                tau = m4 * 4 + a4
                                if g == 0:
                                    st0, stp = 128 * tau, 1
                                elif g == 1:
                                    st0, stp = 512 * (tau % 4) + (tau // 4), 4
                                else:
                                    st0, stp = tau, 16
                                for k in range(8):
                                    fns.append((lambda k, pa, a4, st0, stp, ws: lambda e: e.matmul(
                                        pa.ap[:, a4 * 128:(a4 + 1) * 128], xnT[:, k, st0:st0 + 127 * stp + 1:stp], ws[2].ap[:, k, :], start=(k == 0), stop=(k == 7)))(k, pa, a4, st0, stp, ws))
                            S.group("tensor", fns, reads=[ws[2]] + xn_tbs, writes=[pa])
                            fns = []
                            for a4 in range(4):
                                for jx in range(2):
                                    fns.append((lambda pa, m4, a4, jx, vb: lambda e: e.copy(
                                        vb.ap[:, m4 * 4 + a4, jx, 64 * jx:64 * jx + 64],
                                        pa.ap[:, a4 * 128 + 64 * jx:a4 * 128 + 64 * jx + 64]))(pa, m4, a4, jx, vb))
                            S.group("scalar", fns, reads=[pa], writes=[vb])
                            yield

                    def attn(ui, pgen=None):
                        g, jj = units[ui]
                        qb, kb = qk[ui % 2]
                        vb = Vu[ui % 2]
                        nb = NB[g]
                        ch = 2 * g + jj
                        steps = [(j2, tau) for j2 in range(2) for tau in range(16)]
                        pts = {}
                        ost = {"cur": None, "nxt": None}

                        def stage_s(si):
                            j2, tau = steps[si]
                            ro = slice(64 * j2, 64 * j2 + 64)
                            has_next = (tau % nb) != nb - 1
                            n = 256 if has_next else 128
                            sp = sring.next()
                            S.group("tensor", [
                                (lambda sp, tau, n, kb, qb, ro: lambda e: e.matmul(sp.ap[:, 0:n], kb.ap[ro, tau * 128:(tau + 1) * 128], qb.ap[ro, tau * 128:tau * 128 + n], start=True, stop=False))(sp, tau, n, kb, qb, ro),
                                (lambda sp, n: lambda e: e.matmul(sp.ap[:, 0:n], identb[:, :], maskB[:, 0:n], start=False, stop=True))(sp, n)],
                                reads=[kb, qb, CONST], writes=[sp])
                            pt = ptr.next()
                            S.op("scalar", (lambda pt, sp, n: lambda e: e.activation(pt.ap[:, 0:n], sp.ap[:, 0:n], AF.Exp, scale=0.125))(pt, sp, n), reads=[sp], writes=[pt])
                            pts[si] = pt

                        def stage_p(si):
                            j2, tau = steps[si]
                            base = 64 * j2
                            ro = slice(base, base + 64)
                            rl = slice(64 - base, 128 - base)
                            has_next = (tau % nb) != nb - 1
                            has_prev = (tau % nb) != 0
                            pt = pts.pop(si)
                            if tau == 0:
                                ost["cur"] = None
                                ost["nxt"] = None
                            if tau % 4 == 0:
                                ost["cur"] = ost["nxt"] if ost["nxt"] is not None else oring.next()
                                ost["nxt"] = None
                            cur = ost["cur"]
                            a4 = tau % 4
                            fns = [(lambda cur, pt, tau, a4, has_prev, vb, j2: lambda e: e.matmul(cur.ap[:, a4 * 128:(a4 + 1) * 128], vb.ap[:, tau, j2, :], pt.ap[:, 0:128], start=(not has_prev), stop=True))(cur, pt, tau, a4, has_prev, vb, j2)]
                            wr = [cur]
                            if has_next:
                                if a4 == 3:
                                    ost["nxt"] = oring.next()
                                    tgt, b4 = ost["nxt"], 0
                                else:
                                    tgt, b4 = cur, a4 + 1
                                fns.append((lambda tgt, pt, tau, b4, vb, j2: lambda e: e.matmul(tgt.ap[:, b4 * 128:(b4 + 1) * 128], vb.ap[:, tau, j2, :], pt.ap[:, 128:256], start=True, stop=False))(tgt, pt, tau, b4, vb, j2))
                                if tgt is not cur:
                                    wr.append(tgt)
                            S.group("tensor", fns, reads=[pt, vb], writes=wr)
                            if a4 == 3:
                                m4 = tau // 4
                                if g == 0:
                                    od = oT[ro, ch, m4 * 512:(m4 + 1) * 512]
                                    ld = LS[rl, jj, m4 * 512:(m4 + 1) * 512]
                                    oi = cur.ap[ro, :]
                                    li = cur.ap[rl, :]
                                elif g == 1:
                                    od = oT[ro, ch, m4:T:4]
                                    ld = LS[rl, jj, m4:T:4]
                                    oi = cur.ap[ro, :]
                                    li = cur.ap[rl, :]
                                else:
                                    od = oT[ro, ch, :].rearrange("p (l r) -> p r l", r=16)[:, m4 * 4:(m4 + 1) * 4, :]
                                    ld = LS[rl, jj, :].rearrange("p (l r) -> p r l", r=16)[:, m4 * 4:(m4 + 1) * 4, :]
                                    oi = cur.ap[ro, :].rearrange("p (a l) -> p a l", a=4)
                                    li = cur.ap[rl, :].rearrange("p (a l) -> p a l", a=4)
                                S.op("scalar", (lambda od, oi: lambda e: e.copy(od, oi))(od, oi), reads=[cur], writes=[o_tb])
                                if g == 0:
                                    S.op("vector", (lambda ld, li: lambda e: e.tensor_copy(ld, li))(ld, li), reads=[cur], writes=[ls_tb])
                                else:
                                    S.op("vector", (lambda ld, li: lambda e: e.tensor_tensor(ld, ld, li, ALU.add))(ld, li), reads=[cur], writes=[ls_tb])

                        ns = len(steps)
                        for si in range(ns + LAG):
                            if si < ns:
                                stage_s(si)
                            if si - LAG >= 0:
                                stage_p(si - LAG)
                            if pgen is not None and si % 5 in (1, 3):
                                next(pgen, None)
                        if pgen is not None:
                            for _ in pgen:
                                pass

                    wcur = load_unit(0)
                    for _ in proj(0, wcur):
                        pass
                    wn = load_unit(1)
                    for ui in range(len(units)):
                        if ui + 1 < len(units):
                            attn(ui, proj(ui + 1, wn))
                            if ui + 2 < len(units):
                                wn = load_unit(ui + 2)
                        else:
                            attn(ui)
                    S.flush()
                S.barrier()
                with ExitStack() as os_:
                    recr = Ring([sb(os_, "a_rec%d" % i, [128, 2, 512], F32) for i in range(2)])
                    yring = Ring([sb(os_, "a_y%d" % i, [128, 8, 512], F32) for i in range(2)])
                    rsd = Resid(os_, "ar", l, 1, hsrc, hdst, depth=2, depth_sq=1, nxt=(l, 2, xn_next[0], xn_next[1]))
                    pring = Ring([None] * 4)
                    pring.bufs = [psb[i] for i in range(4)]
                    ybs = {}

                    def mm_stage(tb):
                        tsl = slice(tb * 512, (tb + 1) * 512)
                        rc = recr.next()
                        S.op("scalar", (lambda tsl, rc: lambda e: e.activation(rc.ap[0:64, :, :], LS[64:128, :, tsl], AF.Ln))(tsl, rc), reads=[ls_tb], writes=[rc])
                        S.op("scalar", (lambda tsl, rc: lambda e: e.activation(rc.ap[64:128, :, :], LS[0:64, :, tsl], AF.Ln))(tsl, rc), reads=[ls_tb], writes=[rc])
                        S.op("scalar", (lambda rc: lambda e: e.activation(rc.ap[:, :, :], rc.ap[:, :, :], AF.Exp, scale=-1.0))(rc), reads=[rc], writes=[rc])
                        for g in range(3):
                            S.op("gpsimd", (lambda g, tsl, rc: lambda e: e.tensor_tensor(oT[:, 2 * g:2 * g + 2, tsl], oT[:, 2 * g:2 * g + 2, tsl], rc.ap[:, :, :], ALU.mult))(g, tsl, rc), reads=[rc], writes=[o_tb])
                        yb = yring.next()
                        ybs[tb] = yb
                        for oc in range(8):
                            p = pring.next()
                            S.group("tensor", [(lambda k, p, oc, tsl: lambda e: e.matmul(p.ap[:, :], wo[:, k, oc * 128:(oc + 1) * 128], oT[:, k, tsl], start=(k == 0), stop=(k == 5)))(k, p, oc, tsl) for k in range(6)],
                                    reads=[WO, o_tb], writes=[p])
                            if oc % 2 == 0:
                                S.op("scalar", (lambda p, oc, yb: lambda e: e.copy(yb.ap[:, oc, :], p.ap[:, :]))(p, oc, yb), reads=[p], writes=[yb])
                            else:
                                S.op("vector", (lambda p, oc, yb: lambda e: e.tensor_copy(yb.ap[:, oc, :], p.ap[:, :]))(p, oc, yb), reads=[p], writes=[yb])

                    def ra(tb):
                        rsd.step_a(tb, ybs[tb].ap[:, :, :], ybs[tb])
                    mm_stage(0); mm_stage(1); ra(0); mm_stage(2); rsd.step_b(0); ra(1); mm_stage(3); rsd.step_b(1); ra(2); rsd.step_b(2); ra(3); rsd.step_b(3)
                    S.flush()
                S.barrier()

        def kvf_phase(hsrc):
            with ExitStack() as fs:
                xnT = sb(fs, "k_xn", [128, 8, T], BF16)
                xn_tbs = [TB() for _ in range(NTB)]
                with ExitStack() as ns:
                    norm_phase(ns, hsrc, DEPTH, 0, xn_tbs, xnT)
                    S.flush()
                S.barrier()
                with ExitStack() as ks:
                    wl = WLoader(ks, "wk", [8, 128], 2, 3, eng=("scalar", "vector"))
                    kst = Ring([sb(ks, "k_st%d" % i, [64, T], BF16) for i in range(2)])
                    vst = Ring([sb(ks, "k_vs%d" % i, [128, 16, 128], BF16) for i in range(2)])
                    pring = Ring([None] * 4)
                    pring.bufs = [psb[i] for i in range(4)]
                    for pc in range(8):
                        w = wl.load(wk_d[pc])
                        for j2 in range(2):
                            ks_ = kst.next()
                            for tb in range(NTB):
                                p = pring.next()
                                tsl = slice(tb * 512, (tb + 1) * 512)
                                S.group("tensor", [(lambda k, p, w, tsl, j2: lambda e: e.matmul(p.ap[0:64, :], w.ap[:, k, 64 * j2:64 * j2 + 64], xnT[:, k, tsl], start=(k == 0), stop=(k == 7)))(k, p, w, tsl, j2) for k in range(8)],
                                        reads=[w, xn_tbs[tb]], writes=[p])
                                S.op("scalar", (lambda ks_, p, tsl: lambda e: e.copy(ks_.ap[:, tsl], p.ap[0:64, :]))(ks_, p, tsl), reads=[p], writes=[ks_])
                            S.dma("kst%d" % kst.i, (lambda ks_, h: lambda e: e.dma_start(out=Kd[h], in_=ks_.ap[:, :]))(ks_, 2 * pc + j2), reads=[ks_], eng="gpsimd")
                        w = wl.load(wv_d[pc])
                        vs_ = vst.next()
                        for m4 in range(4):
                            p = pring.next()
                            fns = []
                            for a4 in range(4):
                                tau = m4 * 4 + a4
                                for k in range(8):
                                    fns.append((lambda k, p, w, a4, tau: lambda e: e.matmul(p.ap[:, a4 * 128:(a4 + 1) * 128], xnT[:, k, tau * 128:(tau + 1) * 128], w.ap[:, k, :], start=(k == 0), stop=(k == 7)))(k, p, w, a4, tau))
                            S.group("tensor", fns, reads=[w] + xn_tbs, writes=[p])
                            S.op("scalar", (lambda vs_, p, m4: lambda e: e.copy(vs_.ap[:, m4 * 4:(m4 + 1) * 4, :], p.ap[:, :].rearrange("p (a d) -> p a d", a=4)))(vs_, p, m4), reads=[p], writes=[vs_])
                        for j2 in range(2):
                            S.dma("vst%d_%d" % (vst.i, j2), (lambda vs_, h, j2: lambda e: e.dma_start(out=Vd[h], in_=vs_.ap[:, :, 64 * j2:64 * j2 + 64]))(vs_, 2 * pc + j2, j2), reads=[vs_], eng="gpsimd")
                    wfs = sb(ks, "k_wfs", [128, 8, 16], F32)
                    wfb = sb(ks, "k_wfb", [128, 8, 16], BF16)
                    WF = TB()
                    S.dma("wf", lambda e: e.dma_start(out=wfs[:, :, :], in_=wf_d[:, :, :]), writes=[WF])
                    S.op("gpsimd", lambda e: e.tensor_copy(wfb[:, :, :], wfs[:, :, :]), reads=[WF], writes=[WF])
                    lf = sb(ks, "k_lf", [16, T], F32)
                    cT = sb(ks, "k_cT", [16, T], F32)
                    r1 = sb(ks, "k_r1", [16, T], F32)
                    onesf = sb(ks, "k_on", [16, T], F32)
                    cq3 = sb(ks, "k_cq", [16, 3, T], BF16)
                    LF = TB()
                    S.op("gpsimd", lambda e: e.memset(onesf[:, :], 1.0), writes=[LF])
                    for tb in range(NTB):
                        p = pring.next()
                        tsl = slice(tb * 512, (tb + 1) * 512)
                        S.group("tensor", [(lambda k, p, tsl: lambda e: e.matmul(p.ap[0:16, :], wfb[:, k, :], xnT[:, k, tsl], start=(k == 0), stop=(k == 7)))(k, p, tsl) for k in range(8)],
                                reads=[WF, xn_tbs[tb]], writes=[p])
                        S.op("scalar", (lambda p, tsl: lambda e: e.activation(lf[:, tsl], p.ap[0:16, :], AF.Exp, bias=bfneg[:, 0:1], scale=-1.0))(p, tsl), reads=[p, CONST], writes=[LF])
                        S.op("scalar", (lambda tsl: lambda e: e.activation(lf[:, tsl], lf[:, tsl], AF.Ln, bias=onec[0:16, 0:1], scale=1.0))(tsl), reads=[LF], writes=[LF])
                    S.op("vector", lambda e: e.tensor_tensor_scan(cT[:, :], onesf[:, :], lf[:, :], 0.0, ALU.mult, ALU.subtract), reads=[LF], writes=[LF])
                    S.op("vector", lambda e: e.tensor_scalar(r1[:, :], cT[:, :], 8.0, None, ALU.mult), reads=[LF], writes=[LF])
                    S.op("vector", lambda e: e.tensor_copy(cq3[:, 0, :], r1[:, :]), reads=[LF], writes=[LF])
                    S.op("vector", lambda e: e.tensor_tensor(r1[:, :], r1[:, :], cq3[:, 0, :], ALU.subtract), reads=[LF], writes=[LF])
                    S.op("vector", lambda e: e.tensor_copy(cq3[:, 1, :], r1[:, :]), reads=[LF], writes=[LF])
                    S.op("vector", lambda e: e.tensor_tensor(r1[:, :], r1[:, :], cq3[:, 1, :], ALU.subtract), reads=[LF], writes=[LF])
                    S.op("vector", lambda e: e.tensor_copy(cq3[:, 2, :], r1[:, :]), reads=[LF], writes=[LF])
                    S.dma("cq", lambda e: e.dma_start(out=Cq[:, :, :], in_=cq3[:, :, :]), reads=[LF])
                    p = pring.next()
                    S.group("tensor", [(lambda tau, p: lambda e: e.transpose(p.ap[:, tau * 16:(tau + 1) * 16], cT[:, tau * 128:(tau + 1) * 128], ident[:, :]))(tau, p) for tau in range(16)],
                            reads=[LF, CONST], writes=[p])
                    S.op("scalar", (lambda p: lambda e: e.mul(ckneg[:, :], p.ap[:, 0:256], -1.0))(p), reads=[p], writes=[CKNEG])
                    S.flush()
                S.barrier()

        def layer_b(l, hsrc, hdst, xn_next):
            j = l - NA
            with ExitStack() as fs:
                oT = sb(fs, "b_o", [128, 8, T], BF16)
                wo = sb(fs, "b_wo", [128, 8, 1024], BF16)
                WO = TB()
                o_tb = TB()
                with ExitStack() as as_:
                    xnT = xn_next[0]
                    xn_tbs = [TB() for _ in range(NTB)]
                    with ExitStack() as ns:
                        norm_phase(ns, hsrc, l, 0, xn_tbs, xnT)
                        S.flush()
                    S.barrier()
                    wsr = Ring([sb(as_, "b_wos%d" % i, [128, 1024], F32) for i in range(4)])
                    for kc in range(8):
                        wst = wsr.next()
                        S.dma("wo%d" % wsr.i, (lambda kc, wst: lambda e: e.dma_start(out=wst.ap[:, :], in_=wob_d[j, :, kc, :]))(kc, wst), writes=[wst])
                        ce = ("gpsimd", "vector", "scalar")[kc % 3]
                        if ce == "scalar":
                            S.op("scalar", (lambda kc, wst: lambda e: e.copy(wo[:, kc, :], wst.ap[:, :]))(kc, wst), reads=[wst], writes=[WO])
                        else:
                            S.op(ce, (lambda kc, wst: lambda e: e.tensor_copy(wo[:, kc, :], wst.ap[:, :]))(kc, wst), reads=[wst], writes=[WO])
                    wl = WLoader(as_, "wq", [8, 128], 2, 2, eng=("vector", "gpsimd"))
                    qa = Ring([sb(as_, "b_q%d" % i, [67, T], BF16) for i in range(4)])
                    ka = Ring([sb(as_, "b_k%d" % i, [67, T], BF16) for i in range(4)])
                    va = [Ring([sb(as_, "b_v%d_%d" % (par, i), [128, 16, 128], BF16) for i in range(2)]) for par in range(2)]
                    for b in ka.bufs:
                        S.op("gpsimd", (lambda b: lambda e: e.memset(b.ap[64:67, :], 1.0))(b), writes=[b])
                    for par in range(2):
                        for b in va[par].bufs:
                            S.op("gpsimd", (lambda b: lambda e: e.memset(b.ap[:, :, :], 1.0))(b), writes=[b])
                    ptr = Ring([sb(as_, "b_pt%d" % i, [128, 512], BF16) for i in range(8)])
                    recr = Ring([sb(as_, "b_rc%d" % i, [128, 512], F32) for i in range(2)])
                    pring = Ring([None] * 2)
                    pring.bufs = [psb[0], psb[1]]
                    sring = Ring([None] * 4)
                    sring.bufs = [psb[2], psb[3], psb[4], psb[7]]
                    oring = Ring([None] * 2)
                    oring.bufs = [psb[5], psb[6]]
                    LAG = 3
                    hd = {}
                    wq = {}

                    def pair_loads(pc):
                        wq[pc] = wl.load(wqb_d[j, pc])
                        for par in range(2):
                            h = 2 * pc + par
                            q = qa.next()
                            kk = ka.next()
                            vv = va[par].next()
                            S.dma("bk%d" % ka.i, (lambda kk, h: lambda e: e.dma_start(out=kk.ap[0:64, :], in_=Kd[h]))(kk, h), writes=[kk])
                            S.dma("bv%d_%d" % (par, va[par].i), (lambda vv, h, par: lambda e: e.dma_start(out=vv.ap[:, :, 64 * par:64 * par + 64], in_=Vd[h]))(vv, h, par), writes=[vv])
                            S.dma("bq%d" % qa.i, (lambda q, h: lambda e: e.dma_start(out=q.ap[64:67, :], in_=Cq[h]))(q, h), writes=[q])
                            hd[h] = (q, kk, vv, par)

                    def pair_proj(pc, tb):
                        w = wq[pc]
                        q0_ = hd[2 * pc][0]
                        q1_ = hd[2 * pc + 1][0]
                        p = pring.next()
                        tsl = slice(tb * 512, (tb + 1) * 512)
                        S.group("tensor", [(lambda k: lambda e: e.matmul(p.ap[:, :], w.ap[:, k, :], xnT[:, k, tsl], start=(k == 0), stop=(k == 7)))(k) for k in range(8)],
                                reads=[w, xn_tbs[tb]], writes=[p])
                        S.op("vector", lambda e: e.tensor_copy(q0_.ap[0:64, tsl], p.ap[0:64, :]), reads=[p], writes=[q0_])
                        S.op("vector", lambda e: e.tensor_copy(q1_.ap[0:64, tsl], p.ap[64:128, :]), reads=[p], writes=[q1_])

                    steps = [(h, G, tau) for h in range(16) for G in range(4) for tau in range(4 * G + 4)]
                    pts = {}
                    obs = {}
                    PROJ_AT = {(2, 0): 0, (2, 4): 1, (3, 0): 2, (3, 6): 3}

                    def stage_s(si):
                        h, G, tau = steps[si]
                        if h % 2 == 1 and h + 1 < 16:
                            pc = (h + 1) // 2
                            if G == 1 and tau == 0:
                                pair_loads(pc)
                            if (G, tau) in PROJ_AT:
                                pair_proj(pc, PROJ_AT[(G, tau)])
                        q, kk, vv, par = hd[h]
                        off = max(0, tau - 4 * G) * 128
                        n = 512 - off
                        q0 = 512 * G + off
                        sp = sring.next()
                        diag = tau >= 4 * G
                        fns_ = [lambda e: e.matmul(sp.ap[:, 0:n], kk.ap[0:67, tau * 128:(tau + 1) * 128], q.ap[0:67, q0:q0 + n], start=True, stop=not diag)]
                        if diag:
                            fns_.append(lambda e: e.matmul(sp.ap[:, 0:128], identb[:, :], maskB[:, 0:128], start=False, stop=True))
                        S.group("tensor", fns_, reads=[kk, q, CONST], writes=[sp])
                        pt = ptr.next()
                        S.op("scalar", lambda e: e.activation(pt.ap[:, 0:n], sp.ap[:, 0:n], AF.Exp, bias=ckneg[:, tau * 16 + h:tau * 16 + h + 1], scale=0.125),
                             reads=[sp, CKNEG], writes=[pt])
                        pts[si] = pt

                    def stage_p(si):
                        h, G, tau = steps[si]
                        q, kk, vv, par = hd[h]
                        ro = slice(64 * par, 64 * par + 64)
                        rl = slice(64 - 64 * par, 128 - 64 * par)
                        off = max(0, tau - 4 * G) * 128
                        n = 512 - off
                        last = 4 * G + 3
                        pt = pts.pop(si)
                        if tau == 0:
                            obs[(h, G)] = oring.next()
                        ob = obs[(h, G)]
                        S.op("tensor", lambda e: e.matmul(ob.ap[:, off:512], vv.ap[:, tau, :], pt.ap[:, 0:n], start=(tau == 0), stop=(tau == last)),
                             reads=[pt, vv], writes=[ob])
                        if tau == last:
                            rc = recr.next()
                            S.op("vector", lambda e: e.reciprocal(rc.ap[ro, :], ob.ap[rl, :]), reads=[ob], writes=[rc])
                            S.op("vector", lambda e: e.tensor_tensor(oT[ro, h // 2, G * 512:(G + 1) * 512], ob.ap[ro, :], rc.ap[ro, :], ALU.mult), reads=[ob, rc], writes=[o_tb])

                    pair_loads(0)
                    for tb in range(NTB):
                        pair_proj(0, tb)
                    ns_ = len(steps)
                    for si in range(ns_ + LAG):
                        if si < ns_:
                            stage_s(si)
                        if si - LAG >= 0:
                            stage_p(si - LAG)
                    S.flush()
                S.barrier()
                with ExitStack() as os_:
                    yring = Ring([sb(os_, "b_y%d" % i, [128, 8, 512], F32) for i in range(2)])
                    rsd = Resid(os_, "br", l, 1, hsrc, hdst, depth=2, depth_sq=1, nxt=(l, 2, xn_next[0], xn_next[1]))
                    pring = Ring([None] * 4)
                    pring.bufs = [psb[i] for i in range(4)]
                    ybs = {}

                    def mm_stage(tb):
                        tsl = slice(tb * 512, (tb + 1) * 512)
                        yb = yring.next()
                        ybs[tb] = yb
                        for oc in range(8):
                            p = pring.next()
                            S.group("tensor", [(lambda k, p, oc, tsl: lambda e: e.matmul(p.ap[:, :], wo[:, k, oc * 128:(oc + 1) * 128], oT[:, k, tsl], start=(k == 0), stop=(k == 7)))(k, p, oc, tsl) for k in range(8)],
                                    reads=[WO, o_tb], writes=[p])
                            if oc % 2 == 0:
                                S.op("scalar", (lambda p, oc, yb: lambda e: e.copy(yb.ap[:, oc, :], p.ap[:, :]))(p, oc, yb), reads=[p], writes=[yb])
                            else:
                                S.op("vector", (lambda p, oc, yb: lambda e: e.tensor_copy(yb.ap[:, oc, :], p.ap[:, :]))(p, oc, yb), reads=[p], writes=[yb])

                    def ra(tb):
                        rsd.step_a(tb, ybs[tb].ap[:, :, :], ybs[tb])
                    mm_stage(0); mm_stage(1); ra(0); mm_stage(2); rsd.step_b(0); ra(1); mm_stage(3); rsd.step_b(1); ra(2); rsd.step_b(2); ra(3); rsd.step_b(3)
                    S.flush()
                S.barrier()

        stages = []
        for l in range(DEPTH):
            if l == NA:
                stages.append(("kvf", l))
            stages.append(("mix", l))
            stages.append(("ffn", l))
        if stop is not None:
            stages = stages[:stop]
        n_h = sum(1 for s in stages if s[0] != "kvf")
        cur = xT
        hi = 0
        pp = [hA, hB]
        with ExitStack() as ls_:
            xn_f = None
            for (kind, l) in stages:
                if kind == "kvf":
                    kvf_phase(cur)
                    continue
                hi += 1
                dst = outT if hi == n_h else pp[hi % 2]
                if kind == "mix":
                    ls_.close()
                    xn_f = (sb(ls_, "xn_f", [128, 8, T], BF16), [TB() for _ in range(NTB)])
                    if l < NA:
                        layer_a(l, cur, dst, xn_f)
                    else:
                        layer_b(l, cur, dst, xn_f)
                else:
                    ffn(l, cur, dst, xn_f)
                cur = dst
        S.flush(final=True)
    return nc


def _tile_w(w, kc):
    n = w.shape[1]
    return np.ascontiguousarray(w.reshape(kc, 128, n).transpose(1, 0, 2))


def prep_shared(inputs):
    f = lambda a: np.asarray(a, dtype=np.float32)
    g = f(inputs["norm_gains"])
    kvn = f(inputs["kv_norm"])
    gc = np.zeros((128, 136), np.float32)
    for l in range(DEPTH):
        for i in range(4):
            gc[:, (l * 4 + i) * 8:(l * 4 + i) * 8 + 8] = g[l, i].reshape(8, 128).T
    gc[:, 128:136] = kvn.reshape(8, 128).T
    wqkv = f(inputs["w_qkv_a"])
    perm = np.arange(64)
    perm[0:8] = np.arange(8, 16)
    perm[8:16] = np.arange(0, 8)
    wq_t = np.zeros((NA, 6, 5, 128, 8, 128), np.float32)
    for l in range(NA):
        for gg in range(3):
            for jj in range(2):
                u = gg * 2 + jj
                c0 = (gg * 4 + jj * 2) * 64
                for pi, base in ((0, 0), (2, 768), (4, 1536)):
                    blk = wqkv[l][:, base + c0:base + c0 + 128]
                    wq_t[l, u, pi] = _tile_w(blk, 8)
                    if pi < 4:
                        sw = blk.reshape(1024, 2, 64)[:, :, perm].reshape(1024, 128)
                        wq_t[l, u, pi + 1] = _tile_w(sw, 8)
    woa = np.stack([_tile_w(f(inputs["w_o_a"])[l], 6) for l in range(NA)])
    wqb_full = f(inputs["w_q_b"])
    wqb = np.stack([np.stack([_tile_w(wqb_full[j][:, pc * 128:(pc + 1) * 128], 8) for pc in range(8)]) for j in range(2)])
    wob = np.stack([_tile_w(f(inputs["w_o_b"])[j], 8) for j in range(2)])
    wkvf = f(inputs["w_kvf"])
    wk = np.stack([_tile_w(wkvf[:, pc * 128:(pc + 1) * 128], 8) for pc in range(8)])
    wv = np.stack([_tile_w(wkvf[:, 1024 + pc * 128:1024 + (pc + 1) * 128], 8) for pc in range(8)])
    wf = _tile_w(wkvf[:, 2048:2064], 8)
    wup_full = f(inputs["w_up"])
    wup = np.zeros((DEPTH, NFC, 128, 8, 256), np.float32)
    for l in range(DEPTH):
        for fc in range(NFC):
            wup[l, fc, :, :, 0:128] = _tile_w(wup_full[l][:, fc * 128:(fc + 1) * 128], 8)
            wup[l, fc, :, :, 128:256] = _tile_w(wup_full[l][:, DFF + fc * 128:DFF + (fc + 1) * 128], 8)
    cw = f(inputs["conv_w"])
    cb = f(inputs["conv_b"])
    cpm = np.zeros((128, DEPTH, 44, 4), np.float32)
    for l in range(DEPTH):
        for k in range(3):
            cpm[:, l, :, k] = cw[l, k].reshape(44, 128).T
        cpm[:, l, :, 3] = cb[l].reshape(44, 128).T
    wdn_full = f(inputs["w_down"])
    wdn = np.stack([np.stack([_tile_w(wdn_full[l][:, oc * 128:(oc + 1) * 128], NFC) for oc in range(8)]) for l in range(DEPTH)])
    pos = np.arange(T, dtype=np.float32)
    inv = (np.float32(500000.0) ** (-(np.arange(0, 16, 2, dtype=np.float32)) / np.float32(16))).astype(np.float32)
    ang = (pos[:, None] * inv[None, :]).astype(np.float32)
    cos = np.cos(ang.astype(np.float64)).astype(np.float32).T
    sin = np.sin(ang.astype(np.float64)).astype(np.float32).T
    rc = np.ones((128, T), np.float32)
    rs = np.zeros((128, T), np.float32)
    for hb in (0, 64):
        rc[hb:hb + 8] = cos
        rc[hb + 8:hb + 16] = cos
        rs[hb:hb + 8] = -sin
        rs[hb + 8:hb + 16] = sin
    permR = np.zeros((128, 128), np.float32)
    for m in range(128):
        mh = m % 64
        if mh < 8:
            permR[m + 8, m] = 1.0
        elif mh < 16:
            permR[m - 8, m] = 1.0
    kk = np.arange(128)[:, None]
    qq = np.arange(128)[None, :]
    maskA = np.concatenate([(qq >= kk), (kk >= qq)], axis=1).astype(np.float32)
    return {
        "gcols": gc, "wqkv": wq_t, "woa": woa, "wqb": wqb, "wob": wob, "wk": wk, "wv": wv, "wf": wf,
        "bfneg": np.ascontiguousarray(-f(inputs["b_f"]).reshape(16, 1)),
        "wup": wup, "cp": np.ascontiguousarray(cpm.reshape(128, DEPTH * 44 * 4)), "wdn": wdn,
        "ropeC": rc, "ropeS": rs, "maskA": maskA, "ident": np.eye(16, dtype=np.float32),
        "ident128": np.eye(128, dtype=np.float32), "permR": permR,
    }


_NC_CACHE = {}


def kernel(**inputs):
    x = np.asarray(inputs["x"], dtype=np.float32)
    shared = prep_shared(inputs)
    if "nc" not in _NC_CACHE:
        _NC_CACHE["nc"] = build()
    nc = _NC_CACHE["nc"]
    in_maps = []
    for b in range(8):
        m = dict(shared)
        m["xT"] = np.ascontiguousarray(x[b].T)
        in_maps.append(m)
    res = run_bass_kernel_spmd(nc, in_maps, core_ids=list(range(8)))
    out = np.stack([np.ascontiguousarray(res.results[b]["outT"].T) for b in range(8)])
    return out.astype(np.float32)
```

```python
import numpy as np
from contextlib import ExitStack
import concourse.bass as bass
import concourse.mybir as mybir
from concourse.bass_utils import run_bass_kernel_spmd

F32 = mybir.dt.float32
BF16 = mybir.dt.bfloat16
AF = mybir.ActivationFunctionType
ALU = mybir.AluOpType

D = 1024
T = 2048
DEPTH = 4
NA = 2
DFF = 2816
NFC = 22
NTB = 4
EPS = 1e-6
ENGS = ("sync", "scalar", "gpsimd", "vector", "tensor")
import os as _os
SAME_ENGINE_WAITS = _os.environ.get("SEW", "0") == "1"


class TB:
    def __init__(self, ap=None):
        self.ap = ap
        self.w = {}
        self.r = {}


def _merge(d, tk):
    s, v = tk
    k = id(s)
    if k not in d or d[k][1] < v:
        d[k] = (s, v)


class Sched:
    def __init__(self, nc, es):
        self.nc = nc
        self.es = es
        self.ops = {e: [] for e in ENGS}
        self.esem = {}
        self.ecnt = {}
        for e in ("scalar", "gpsimd", "vector", "tensor"):
            self.esem[e] = es.enter_context(nc.semaphore("s_" + e))
            self.ecnt[e] = 0
        self.dsem = {}
        self.dcnt = {}
        self.seen = {e: {} for e in ENGS}
        self.bar = {e: [] for e in ENGS}
        self.nops = 0

    def _deps(self, eng, reads, writes, waits):
        d = {}
        for b in reads:
            for tk in b.w.values():
                _merge(d, tk)
        for b in writes:
            for tk in b.w.values():
                _merge(d, tk)
            for tk in b.r.values():
                _merge(d, tk)
        for tk in waits:
            if tk is not None:
                _merge(d, tk)
        for tk in self.bar[eng]:
            _merge(d, tk)
        self.bar[eng] = []
        out = []
        for (s, v) in d.values():
            if eng in self.esem and s is self.esem[eng]:
                if eng == "tensor" or not SAME_ENGINE_WAITS:
                    continue
            out.append((s, v))
        return out

    def _commit(self, tk, reads, writes):
        for b in writes:
            b.w = {}
            b.r = {}
            _merge(b.w, tk)
        for b in reads:
            _merge(b.r, tk)

    def op(self, eng, fn, reads=(), writes=(), waits=()):
        return self.group(eng, [fn], reads, writes, waits)

    def group(self, eng, fns, reads=(), writes=(), waits=()):
        deps = self._deps(eng, reads, writes, waits)
        self.ecnt[eng] += 1
        tk = (self.esem[eng], self.ecnt[eng])
        n = len(fns)
        for i, fn in enumerate(fns):
            self.ops[eng].append((fn, tuple(deps) if i == 0 else (), (self.esem[eng], 1) if i == n - 1 else None))
        self.nops += n
        self._commit(tk, reads, writes)
        return tk

    def dma(self, key, fn, reads=(), writes=(), waits=(), eng="sync"):
        deps = self._deps(eng, reads, writes, waits)
        if key not in self.dsem:
            self.dsem[key] = self.es.enter_context(self.nc.semaphore("d_" + key))
            self.dcnt[key] = 0
        self.dcnt[key] += 16
        tk = (self.dsem[key], self.dcnt[key])
        self.ops[eng].append((fn, tuple(deps), (self.dsem[key], 16)))
        self.nops += 1
        self._commit(tk, reads, writes)
        return tk

    def all_tickets(self):
        tks = [(self.esem[e], self.ecnt[e]) for e in self.esem if self.ecnt[e] > 0]
        tks += [(self.dsem[k], self.dcnt[k]) for k in self.dsem]
        return tks

    def barrier(self):
        tks = self.all_tickets()
        for e in ENGS:
            self.bar[e] = list(tks)

    def flush(self, final=False):
        nc = self.nc
        fin = self.all_tickets() if final else []
        with nc.Block() as block:
            def mk(ename):
                def body(engine):
                    seen = self.seen[ename]
                    for fn, waits, inc in self.ops[ename]:
                        for (s, v) in waits:
                            k = id(s)
                            if seen.get(k, 0) >= v:
                                continue
                            seen[k] = v
                            engine.wait_ge(s, v)
                        ins = fn(engine)
                        if inc is not None:
                            ins.then_inc(inc[0], inc[1])
                    if ename == "sync":
                        for (s, v) in fin:
                            engine.wait_ge(s, v)
                return body
            block.sync(mk("sync"))
            block.scalar(mk("scalar"))
            block.gpsimd(mk("gpsimd"))
            block.vector(mk("vector"))
            block.tensor(mk("tensor"))
        self.ops = {e: [] for e in ENGS}


class Ring:
    def __init__(self, aps):
        self.bufs = [TB(a) for a in aps]
        self.i = -1

    def next(self):
        self.i = (self.i + 1) % len(self.bufs)
        return self.bufs[self.i]


def build(stop=None, dbg=None):
    nc = bass.Bass("TRN2", target_bir_lowering=False)

    def din(name, shape, dt=F32):
        return nc.dram_tensor(name, list(shape), dt, kind="ExternalInput").ap()

    xT = din("xT", [D, T])
    gcols_d = din("gcols", [128, 136])
    wqkv_d = din("wqkv", [NA, 6, 5, 128, 8, 128])
    woa_d = din("woa", [NA, 128, 6, 1024])
    wqb_d = din("wqb", [2, 8, 128, 8, 128])
    wob_d = din("wob", [2, 128, 8, 1024])
    wk_d = din("wk", [8, 128, 8, 128])
    wv_d = din("wv", [8, 128, 8, 128])
    wf_d = din("wf", [128, 8, 16])
    bfneg_d = din("bfneg", [16, 1])
    wup_d = din("wup", [DEPTH, NFC, 128, 8, 256])
    cp_d = din("cp", [128, DEPTH * 44 * 4])
    wdn_d = din("wdn", [DEPTH, 8, 128, NFC, 128])
    ropeC_d = din("ropeC", [128, T])
    ropeS_d = din("ropeS", [128, T])
    maskA_d = din("maskA", [128, 256])
    ident_d = din("ident", [16, 16])
    ident128_d = din("ident128", [128, 128])
    permR_d = din("permR", [128, 128])
    outT = nc.dram_tensor("outT", [D, T], F32, kind="ExternalOutput").ap()

    hA = nc.dram_tensor("hA", [D, T], F32).ap()
    hB = nc.dram_tensor("hB", [D, T], F32).ap()
    Kd = nc.dram_tensor("Kd", [16, 64, T], BF16).ap()
    Vd = nc.dram_tensor("Vd", [16, 128, 16, 64], BF16).ap()
    Cq = nc.dram_tensor("Cq", [16, 3, T], BF16).ap()

    def hview(h):
        return h.rearrange("(c p) t -> p c t", p=128)

    with ExitStack() as es:
        S = Sched(nc, es)

        uniq = [0]
        dumps = {}

        def dump(name, ap, shape, dt, reads):
            if dbg is None or name not in dbg or name in dumps:
                return
            d = nc.dram_tensor("dbg_" + name, list(shape), dt, kind="ExternalOutput").ap()
            dumps[name] = d
            S.dma("dbg_" + name, lambda e: e.dma_start(out=d, in_=ap), reads=reads, eng="gpsimd")

        def sb(scope, name, shape, dt):
            uniq[0] += 1
            return scope.enter_context(nc.sbuf_tensor("sb%d_%s" % (uniq[0], name), list(shape), dt))

        banks = [es.enter_context(nc.psum_tensor("bank%d" % i, [128, 512], F32)) for i in range(8)]
        psb = [TB(b) for b in banks]

        gcols = sb(es, "gcols", [128, 136], F32)
        cp = sb(es, "cp", [128, DEPTH * 44 * 4], F32)
        ones_bf = sb(es, "ones_bf", [128, 128], BF16)
        mask_f = sb(es, "mask_f", [128, 256], F32)
        maskA = sb(es, "maskA", [128, 256], BF16)
        epsc = sb(es, "epsc", [128, 1], F32)
        onec = sb(es, "onec", [128, 1], F32)
        ident = sb(es, "ident", [16, 16], F32)
        bfneg = sb(es, "bfneg", [16, 1], F32)
        ckneg = sb(es, "ckneg", [128, 256], F32)
        id_f = sb(es, "id_f", [128, 128], F32)
        identb = sb(es, "identb", [128, 128], BF16)
        maskB = sb(es, "maskB", [128, 256], BF16)
        permb = sb(es, "permb", [128, 128], BF16)
        CONST = TB()
        CKNEG = TB()

        S.dma("c0", lambda e: e.dma_start(out=gcols[:, :], in_=gcols_d[:, :]), writes=[CONST])
        S.dma("c0", lambda e: e.dma_start(out=cp[:, :], in_=cp_d[:, :]), writes=[CONST])
        S.dma("c0", lambda e: e.dma_start(out=mask_f[:, :], in_=maskA_d[:, :]), writes=[CONST])
        S.dma("c0", lambda e: e.dma_start(out=ident[:, :], in_=ident_d[:, :]), writes=[CONST])
        S.dma("c0", lambda e: e.dma_start(out=bfneg[:, :], in_=bfneg_d[:, :]), writes=[CONST])
        tc0 = (S.dsem["c0"], S.dcnt["c0"])
        S.op("gpsimd", lambda e: e.memset(ones_bf[:, :], 1.0), writes=[CONST])
        S.op("gpsimd", lambda e: e.memset(epsc[:, :], EPS), writes=[CONST])
        S.op("gpsimd", lambda e: e.memset(onec[:, :], 1.0), writes=[CONST])
        S.op("gpsimd", lambda e: e.tensor_copy(maskA[:, :], mask_f[:, :]), writes=[CONST], waits=[tc0])
        S.dma("c1", lambda e: e.dma_start(out=id_f[:, :], in_=ident128_d[:, :]), writes=[CONST])
        tc1 = (S.dsem["c1"], S.dcnt["c1"])
        S.op("gpsimd", lambda e: e.tensor_copy(identb[:, :], id_f[:, :]), writes=[CONST], waits=[tc1])
        S.dma("c2", lambda e: e.dma_start(out=id_f[:, :], in_=permR_d[:, :]), reads=[CONST], writes=[CONST])
        S.op("gpsimd", lambda e: e.tensor_copy(permb[:, :], id_f[:, :]), reads=[CONST], writes=[CONST])
        S.op("gpsimd", lambda e: e.tensor_scalar(maskB[:, :], mask_f[:, :], -1.0, 30000.0, ALU.add, ALU.mult), writes=[CONST], waits=[tc0])
        S.barrier()

        def gcol(l, i, c):
            k = (l * 4 + i) * 8 + c if l < DEPTH else 128 + c
            return gcols[:, k:k + 1]

        def rstd_from(src_tb, src3, sq_ring, rs_ring, ps_ss):
            sq = sq_ring.next()
            S.op("scalar", lambda e: e.activation(sq.ap[:, :, :], src3, AF.Square), reads=[src_tb], writes=[sq])
            S.group("tensor", [(lambda c: lambda e: e.matmul(ps_ss.ap[:, :], ones_bf[:, :], sq.ap[:, c, :], start=(c == 0), stop=(c == 7)))(c) for c in range(8)],
                    reads=[sq], writes=[ps_ss])
            rs = rs_ring.next()
            S.op("scalar", lambda e: e.activation(rs.ap[:, :], ps_ss.ap[:, :], AF.Ln, bias=epsc[:, 0:1], scale=1.0 / D), reads=[ps_ss], writes=[rs])
            S.op("scalar", lambda e: e.activation(rs.ap[:, :], rs.ap[:, :], AF.Exp, scale=-0.5), reads=[rs], writes=[rs])
            return rs

        def norm_phase(scope, hsrc, l, gi, xn_tbs, xnT):
            hring = Ring([sb(scope, "n_h%d" % i, [128, 8, 512], F32) for i in range(3)])
            sq_ring = Ring([sb(scope, "n_sq%d" % i, [128, 8, 512], BF16) for i in range(2)])
            rs_ring = Ring([sb(scope, "n_rs%d" % i, [128, 512], F32) for i in range(2)])
            ssr = Ring([None] * 2)
            ssr.bufs = [psb[6], psb[7]]
            hv = hview(hsrc)
            st = {}

            def s1(tb):
                hb = hring.next()
                S.dma("nh%d" % hring.i, lambda e: e.dma_start(out=hb.ap[:, :, :], in_=hv[:, :, tb * 512:(tb + 1) * 512]), writes=[hb])
                sq = sq_ring.next()
                S.op("scalar", lambda e: e.activation(sq.ap[:, :, :], hb.ap[:, :, :], AF.Square), reads=[hb], writes=[sq])
                ps = ssr.next()
                S.group("tensor", [(lambda c: lambda e: e.matmul(ps.ap[:, :], ones_bf[:, :], sq.ap[:, c, :], start=(c == 0), stop=(c == 7)))(c) for c in range(8)],
                        reads=[sq], writes=[ps])
                st[tb] = (hb, ps)

            def s2(tb):
                hb, ps = st.pop(tb)
                rs = rs_ring.next()
                S.op("scalar", lambda e: e.activation(rs.ap[:, :], ps.ap[:, :], AF.Ln, bias=epsc[:, 0:1], scale=1.0 / D), reads=[ps], writes=[rs])
                S.op("scalar", lambda e: e.activation(rs.ap[:, :], rs.ap[:, :], AF.Exp, scale=-0.5), reads=[rs], writes=[rs])
                S.group("vector", [(lambda c: lambda e: e.scalar_tensor_tensor(
                    xnT[:, c, tb * 512:(tb + 1) * 512], hb.ap[:, c, :], gcol(l, gi, c), rs.ap[:, :], ALU.mult, ALU.mult))(c) for c in range(8)],
                    reads=[hb, rs], writes=[xn_tbs[tb]])
            s1(0); s1(1); s2(0); s1(2); s2(1); s1(3); s2(2); s2(3)

        class Resid:
            def __init__(self, scope, name, l, gi, hsrc, hdst, depth=2, depth_sq=None, nxt=None):
                depth_sq = depth if depth_sq is None else depth_sq
                self.hring = Ring([sb(scope, "%s_h%d" % (name, i), [128, 8, 512], F32) for i in range(depth)])
                self.sq_ring = Ring([sb(scope, "%s_sq%d" % (name, i), [128, 8, 512], BF16) for i in range(depth_sq)])
                self.rs_ring = Ring([sb(scope, "%s_rs%d" % (name, i), [128, 512], F32) for i in range(2)])
                self.hv = hview(hsrc)
                self.ov = hview(hdst)
                self.l, self.gi = l, gi
                self.nxt = nxt
                self.st_ = {}

            def step(self, tb, y3, ytb):
                self.step_a(tb, y3, ytb)
                self.step_b(tb)

            def step_a(self, tb, y3, ytb):
                l, gi = self.l, self.gi
                hv, ov = self.hv, self.ov
                hb = self.hring.next()
                self.st_[tb] = (hb, self.hring.i)
                S.dma("rh%d" % self.hring.i, lambda e: e.dma_start(out=hb.ap[:, :, :], in_=hv[:, :, tb * 512:(tb + 1) * 512]), writes=[hb])
                rs = rstd_from(ytb, y3, self.sq_ring, self.rs_ring, psb[7])
                S.group("vector", [(lambda c: lambda e: e.scalar_tensor_tensor(
                    y3[:, c, :], y3[:, c, :], gcol(l, gi, c), rs.ap[:, :], ALU.mult, ALU.mult))(c) for c in range(8)],
                    reads=[rs], writes=[ytb])
                S.op("vector", lambda e: e.tensor_tensor(hb.ap[:, :, :], hb.ap[:, :, :], y3, ALU.add), reads=[ytb], writes=[hb])

            def step_b(self, tb):
                ov = self.ov
                hb, hi_ = self.st_.pop(tb)
                if self.nxt is not None:
                    nl, ngi, nxn, nxn_tbs = self.nxt
                    rs2 = rstd_from(hb, hb.ap[:, :, :], self.sq_ring, self.rs_ring, psb[7])
                    S.group("vector", [(lambda c: lambda e: e.scalar_tensor_tensor(
                        nxn[:, c, tb * 512:(tb + 1) * 512], hb.ap[:, c, :], gcol(nl, ngi, c), rs2.ap[:, :], ALU.mult, ALU.mult))(c) for c in range(8)],
                        reads=[hb, rs2], writes=[nxn_tbs[tb]])
                S.dma("ro%d" % hi_, lambda e: e.dma_start(out=ov[:, :, tb * 512:(tb + 1) * 512], in_=hb.ap[:, :, :]), reads=[hb], eng="gpsimd")

        class WLoader:
            def __init__(self, scope, name, shape, nst=2, nbf=2, eng="gpsimd"):
                self.name = name
                self.eng = eng
                self.n = 0
                self.st = Ring([sb(scope, "%s_s%d" % (name, i), [128] + list(shape), F32) for i in range(nst)])
                self.bf = Ring([sb(scope, "%s_b%d" % (name, i), [128] + list(shape), BF16) for i in range(nbf)])

            def load(self, dram_ap):
                st = self.st.next()
                S.dma("%s%d" % (self.name, self.st.i), lambda e: e.dma_start(out=st.ap[:], in_=dram_ap), writes=[st])
                wb = self.bf.next()
                eng = self.eng
                if isinstance(eng, (list, tuple)):
                    eng = eng[self.n % len(eng)]
                    self.n += 1
                if eng == "scalar":
                    S.op("scalar", lambda e: e.copy(wb.ap[:], st.ap[:]), reads=[st], writes=[wb])
                else:
                    S.op(eng, lambda e: e.tensor_copy(wb.ap[:], st.ap[:]), reads=[st], writes=[wb])
                return wb

        def ffn(l, hsrc, hdst, xn_pre):
            with ExitStack() as us:
                uT = sb(us, "f_u", [128, NFC, T], BF16)
                u_tb = TB()
                with ExitStack() as fs:
                    xnT, xn_tbs = xn_pre
                    with ExitStack() as ps_:
                        wl = WLoader(ps_, "wu", [8, 256], 2, 2, eng="scalar")
                        abuf = Ring([sb(ps_, "f_a%d" % i, [128, 2, 514], F32) for i in range(3)])
                        acc = Ring([sb(ps_, "f_c%d" % i, [128, 2, 512], F32) for i in range(4)])
                        pring = Ring([None] * 6)
                        pring.bufs = [psb[i] for i in range(6)]

                        def cpc(ch, k):
                            o = ((l * 44 + ch) * 4) + k
                            return cp[:, o:o + 1]
                        iters = [(fc, tb) for fc in range(NFC) for tb in range(NTB)]
                        st = {}
                        wst_ = {"next": wl.load(wup_d[l, 0]), "cur": None}

                        def stage_a(i):
                            fc, tb = iters[i]
                            if tb == 0:
                                wst_["cur"] = wst_["next"]
                                if fc + 1 < NFC:
                                    wst_["next"] = wl.load(wup_d[l, fc + 1])
                            w = wst_["cur"]
                            pg = pring.next()
                            pv = pring.next()
                            tsl = slice(tb * 512, (tb + 1) * 512)
                            S.group("tensor", [(lambda k, pg, w, tsl: lambda e: e.matmul(pg.ap[:, :], w.ap[:, k, 0:128], xnT[:, k, tsl], start=(k == 0), stop=(k == 7)))(k, pg, w, tsl) for k in range(8)],
                                    reads=[w, xn_tbs[tb]], writes=[pg])
                            S.group("tensor", [(lambda k, pv, w, tsl: lambda e: e.matmul(pv.ap[:, :], w.ap[:, k, 128:256], xnT[:, k, tsl], start=(k == 0), stop=(k == 7)))(k, pv, w, tsl) for k in range(8)],
                                    reads=[w, xn_tbs[tb]], writes=[pv])
                            a = abuf.next()
                            c = acc.next()
                            if tb == 0:
                                S.op("gpsimd", (lambda a: lambda e: e.memset(a.ap[:, :, 0:2], 0.0))(a), writes=[a])
                            else:
                                pa = st[i - 1][0]
                                S.op("gpsimd", (lambda a, pa: lambda e: e.tensor_copy(a.ap[:, :, 0:2], pa.ap[:, :, 512:514]))(a, pa), reads=[pa], writes=[a])
                            S.op("scalar", (lambda a, pg: lambda e: e.copy(a.ap[:, 0, 2:514], pg.ap[:, :]))(a, pg), reads=[pg], writes=[a])
                            S.op("scalar", (lambda a, pv: lambda e: e.copy(a.ap[:, 1, 2:514], pv.ap[:, :]))(a, pv), reads=[pv], writes=[a])
                            S.op("scalar", (lambda c, pg, fc: lambda e: e.activation(c.ap[:, 0, :], pg.ap[:, :], AF.Identity, bias=cpc(fc, 3), scale=cpc(fc, 2)))(c, pg, fc), reads=[pg, CONST], writes=[c])
                            S.op("scalar", (lambda c, pv, fc: lambda e: e.activation(c.ap[:, 1, :], pv.ap[:, :], AF.Identity, bias=cpc(22 + fc, 3), scale=cpc(22 + fc, 2)))(c, pv, fc), reads=[pv, CONST], writes=[c])
                            st[i] = (a, c)

                        def stage_b(i):
                            fc, tb = iters[i]
                            a, c = st[i]
                            fns = []
                            for half, ch in ((0, fc), (1, 22 + fc)):
                                for (off, k) in ((1, 1), (0, 0)):
                                    fns.append((lambda c, a, half, ch, off, k: lambda e: e.scalar_tensor_tensor(
                                        c.ap[:, half, :], a.ap[:, half, off:off + 512], cpc(ch, k), c.ap[:, half, :], ALU.mult, ALU.add))(c, a, half, ch, off, k))
                            S.group("vector", fns, reads=[a], writes=[c])

                        def stage_c(i):
                            fc, tb = iters[i]
                            a, c = st[i]
                            tsl = slice(tb * 512, (tb + 1) * 512)
                            S.op("scalar", (lambda c: lambda e: e.activation(c.ap[:, 0, :], c.ap[:, 0, :], AF.Gelu_apprx_tanh))(c), reads=[c], writes=[c])
                            S.op("vector", (lambda c, fc, tsl: lambda e: e.tensor_tensor(uT[:, fc, tsl], c.ap[:, 0, :], c.ap[:, 1, :], ALU.mult))(c, fc, tsl), reads=[c], writes=[u_tb])

                        ni = len(iters)
                        for i in range(ni + 2):
                            if i < ni:
                                stage_a(i)
                            if 0 <= i - 1 < ni:
                                stage_b(i - 1)
                            if 0 <= i - 2 < ni:
                                stage_c(i - 2)
                        S.flush()
                    S.barrier()
                with ExitStack() as ds:
                    yh = xn_pre[0].bitcast(F32)
                    y_tbs = [TB(), TB()]
                    wl = WLoader(ds, "wd", [11, 128], 2, 4, eng=("scalar", "vector"))
                    rsd = Resid(ds, "fr", l, 3, hsrc, hdst, depth=2, depth_sq=1)
                    pring = Ring([None] * 6)
                    pring.bufs = [psb[i] for i in range(6)]
                    NPRE = 6
                    glist = [(th, oc, t2) for th in range(2) for oc in range(8) for t2 in range(2)]
                    wcur = {}

                    def wload(oc):
                        return [wl.load(wdn_d[l, oc, :, 0:11, :]), wl.load(wdn_d[l, oc, :, 11:22, :])]

                    def emit_mm(gi):
                        th, oc, t2 = glist[gi]
                        if t2 == 0:
                            if oc == 0:
                                wcur["w"] = wload(0)
                            else:
                                wcur["w"] = wcur["n"]
                            if oc + 1 < 8:
                                wcur["n"] = wload(oc + 1)
                        w2 = wcur["w"]
                        tb = th * 2 + t2
                        p = pring.next()
                        tsl = slice(tb * 512, (tb + 1) * 512)
                        S.group("tensor", [(lambda k: lambda e: e.matmul(p.ap[:, :], w2[k // 11].ap[:, k % 11, :], uT[:, k, tsl], start=(k == 0), stop=(k == NFC - 1)))(k) for k in range(NFC)],
                                reads=[w2[0], w2[1], u_tb], writes=[p])
                        return p

                    def emit_copy(gi, p):
                        th, oc, t2 = glist[gi]
                        if oc % 2 == 0:
                            S.op("scalar", lambda e: e.copy(yh[:, oc, t2 * 512:(t2 + 1) * 512], p.ap[:, :]), reads=[p], writes=[y_tbs[t2]])
                        else:
                            S.op("vector", lambda e: e.tensor_copy(yh[:, oc, t2 * 512:(t2 + 1) * 512], p.ap[:, :]), reads=[p], writes=[y_tbs[t2]])

                    for gi in range(16):
                        emit_copy(gi, emit_mm(gi))
                    pend = [(gi, emit_mm(gi)) for gi in range(16, 16 + NPRE)]
                    for t2 in range(2):
                        rsd.step(t2, yh[:, :, t2 * 512:(t2 + 1) * 512], y_tbs[t2])
                    for (gi, p) in pend:
                        emit_copy(gi, p)
                    for gi in range(16 + NPRE, 32):
                        emit_copy(gi, emit_mm(gi))
                    for t2 in range(2):
                        rsd.step(2 + t2, yh[:, :, t2 * 512:(t2 + 1) * 512], y_tbs[t2])
                    S.flush()
                S.barrier()

        def layer_a(l, hsrc, hdst, xn_next):
            with ExitStack() as fs:
                oT = sb(fs, "a_o", [128, 6, T], BF16)
                LS = sb(fs, "a_ls", [128, 2, T], F32)
                wo = sb(fs, "a_wo", [128, 6, 1024], BF16)
                WO = TB()
                o_tb = TB()
                ls_tb = TB()
                with ExitStack() as as_:
                    xnT = xn_next[0]
                    xn_tbs = [TB() for _ in range(NTB)]
                    with ExitStack() as ns:
                        norm_phase(ns, hsrc, l, 0, xn_tbs, xnT)
                        S.flush()
                    S.barrier()
                    wsr = Ring([sb(as_, "a_wos%d" % i, [128, 1024], F32) for i in range(3)])
                    for kc in range(6):
                        wst = wsr.next()
                        S.dma("wo%d" % wsr.i, (lambda kc, wst: lambda e: e.dma_start(out=wst.ap[:, :], in_=woa_d[l, :, kc, :]))(kc, wst), writes=[wst])
                        ce = ("gpsimd", "vector", "scalar")[kc % 3]
                        if ce == "scalar":
                            S.op("scalar", (lambda kc, wst: lambda e: e.copy(wo[:, kc, :], wst.ap[:, :]))(kc, wst), reads=[wst], writes=[WO])
                        else:
                            S.op(ce, (lambda kc, wst: lambda e: e.tensor_copy(wo[:, kc, :], wst.ap[:, :]))(kc, wst), reads=[wst], writes=[WO])
                    wl = WLoader(as_, "wa", [8, 128], 2, 5, eng=("scalar", "vector"))
                    ropeC = sb(as_, "a_rc", [128, T], F32)
                    ropeS = sb(as_, "a_rs", [128, T], F32)
                    ROPE = TB()
                    S.dma("rope", lambda e: e.dma_start(out=ropeC[:, :], in_=ropeC_d[:, :]), writes=[ROPE])
                    S.dma("rope", lambda e: e.dma_start(out=ropeS[:, :], in_=ropeS_d[:, :]), writes=[ROPE])
                    qk = [[TB(sb(as_, "a_%s%d" % (n, i), [128, T], BF16)) for n in ("q", "k")] for i in range(2)]
                    Vu = [TB(sb(as_, "a_v%d" % i, [128, 16, 2, 128], BF16)) for i in range(2)]
                    for i in range(2):
                        S.op("gpsimd", (lambda i: lambda e: e.memset(Vu[i].ap[:, :, :, :], 1.0))(i), writes=[Vu[i]])
                    t1r = Ring([sb(as_, "a_t1%d" % i, [128, 512], F32) for i in range(2)])
                    t2r = Ring([sb(as_, "a_t2%d" % i, [128, 512], F32) for i in range(2)])
                    qtr = Ring([sb(as_, "a_qt%d" % i, [128, 512], BF16) for i in range(2)])
                    ptr = Ring([sb(as_, "a_pt%d" % i, [128, 256], BF16) for i in range(6)])
                    pring = Ring([None] * 3)
                    pring.bufs = [psb[i] for i in range(3)]
                    sring = Ring([None] * 3)
                    sring.bufs = [psb[3], psb[4], psb[5]]
                    oring = Ring([None] * 2)
                    oring.bufs = [psb[6], psb[7]]
                    units = [(g, jj) for g in range(3) for jj in range(2)]
                    DIL = [1, 4, 16]
                    NB = [16, 4, 1]
                    LAG = 3

                    def load_unit(u):
                        return [wl.load(wqkv_d[l, u, pc]) for pc in (0, 2, 4)]

                    def proj(ui, ws):
                        g, jj = units[ui]
                        r = DIL[g]
                        qb, kb = qk[ui % 2]
                        vb = Vu[ui % 2]
                        for (wi, dst) in ((0, qb), (1, kb)):
                            for tb in range(NTB):
                                pa = pring.next()
                                pb = pring.next()
                                tsl = slice(tb * 512, (tb + 1) * 512)
                                S.group("tensor", [(lambda k, pa, tsl, wi, ws: lambda e: e.matmul(pa.ap[:, :], ws[wi].ap[:, k, :], xnT[:, k, tsl], start=(k == 0), stop=(k == 7)))(k, pa, tsl, wi, ws) for k in range(8)],
                                        reads=[ws[wi], xn_tbs[tb]], writes=[pa])
                                qt = qtr.next()
                                S.op("scalar", (lambda qt, pa: lambda e: e.copy(qt.ap[:, :], pa.ap[:, :]))(qt, pa), reads=[pa], writes=[qt])
                                S.op("tensor", (lambda pb, qt: lambda e: e.matmul(pb.ap[:, :], permb[:, :], qt.ap[:, :], start=True, stop=True))(pb, qt), reads=[qt, CONST], writes=[pb])
                                t1 = t1r.next()
                                t2 = t2r.next()
                                S.op("vector", (lambda t1, pa, tsl: lambda e: e.tensor_tensor(t1.ap[:, :], pa.ap[:, :], ropeC[:, tsl], ALU.mult))(t1, pa, tsl), reads=[pa, ROPE, qt], writes=[t1])
                                S.op("vector", (lambda t2, pb, tsl: lambda e: e.tensor_tensor(t2.ap[:, :], pb.ap[:, :], ropeS[:, tsl], ALU.mult))(t2, pb, tsl), reads=[pb, ROPE], writes=[t2])
                                if r == 1:
                                    oap = dst.ap[:, tsl]
                                    i1 = t1.ap[:, :]
                                    i2 = t2.ap[:, :]
                                else:
                                    m = 512 // r
                                    oap = dst.ap[:, :].rearrange("p (r l) -> p r l", r=r)[:, :, tb * m:(tb + 1) * m]
                                    i1 = t1.ap[:, :].rearrange("p (m r) -> p r m", r=r)
                                    i2 = t2.ap[:, :].rearrange("p (m r) -> p r m", r=r)
                                S.op("gpsimd", (lambda oap, i1, i2: lambda e: e.tensor_tensor(oap, i1, i2, ALU.add))(oap, i1, i2), reads=[t1, t2], writes=[dst])
                                yield
                        for m4 in range(4):
                            pa = pring.next()
                            fns = []
                            for a4 in range(4):
                                tau = m4 * 4 + a4
                                if g == 0:
                                    st0, stp = 128 * tau, 1
                                elif g == 1:
                                    st0, stp = 512 * (tau % 4) + (tau // 4), 4
                                else:
                                    st0, stp = tau, 16
                                for k in range(8):
                                    fns.append((lambda k, pa, a4, st0, stp, ws: lambda e: e.matmul(
                                        pa.ap[:, a4 * 128:(a4 + 1) * 128], xnT[:, k, st0:st0 + 127 * stp + 1:stp], ws[2].ap[:, k, :], start=(k == 0), stop=(k == 7)))(k, pa, a4, st0, stp, ws))
                            S.group("tensor", fns, reads=[ws[2]] + xn_tbs, writes=[pa])
                            fns = []
                            for a4 in range(4):
                                for jx in range(2):
                                    fns.append((lambda pa, m4, a4, jx, vb: lambda e: e.copy(
                                        vb.ap[:, m4 * 4 + a4, jx, 64 * jx:64 * jx + 64],
                                        pa.ap[:, a4 * 128 + 64 * jx:a4 * 128 + 64 * jx + 64]))(pa, m4, a4, jx, vb))
                            S.group("scalar", fns, reads=[pa], writes=[vb])
                            yield

                    def attn(ui, pgen=None):
                        g, jj = units[ui]
                        qb, kb = qk[ui % 2]
                        vb = Vu[ui % 2]
                        nb = NB[g]
                        ch = 2 * g + jj
                        steps = [(j2, tau) for j2 in range(2) for tau in range(16)]
                        pts = {}
                        ost = {"cur": None, "nxt": None}

                        def stage_s(si):
                            j2, tau = steps[si]
                            ro = slice(64 * j2, 64 * j2 + 64)
                            has_next = (tau % nb) != nb - 1
                            n = 256 if has_next else 128
                            sp = sring.next()
                            S.group("tensor", [
                                (lambda sp, tau, n, kb, qb, ro: lambda e: e.matmul(sp.ap[:, 0:n], kb.ap[ro, tau * 128:(tau + 1) * 128], qb.ap[ro, tau * 128:tau * 128 + n], start=True, stop=False))(sp, tau, n, kb, qb, ro),
                                (lambda sp, n: lambda e: e.matmul(sp.ap[:, 0:n], identb[:, :], maskB[:, 0:n], start=False, stop=True))(sp, n)],
                                reads=[kb, qb, CONST], writes=[sp])
                            pt = ptr.next()
                            S.op("scalar", (lambda pt, sp, n: lambda e: e.activation(pt.ap[:, 0:n], sp.ap[:, 0:n], AF.Exp, scale=0.125))(pt, sp, n), reads=[sp], writes=[pt])
                            pts[si] = pt

                        def stage_p(si):
                            j2, tau = steps[si]
                            base = 64 * j2
                            ro = slice(base, base + 64)
                            rl = slice(64 - base, 128 - base)
                            has_next = (tau % nb) != nb - 1
                            has_prev = (tau % nb) != 0
                            pt = pts.pop(si)
                            if tau == 0:
                                ost["cur"] = None
                                ost["nxt"] = None
                            if tau % 4 == 0:
                                ost["cur"] = ost["nxt"] if ost["nxt"] is not None else oring.next()
                                ost["nxt"] = None
                            cur = ost["cur"]
                            a4 = tau % 4
                            fns = [(lambda cur, pt, tau, a4, has_prev, vb, j2: lambda e: e.matmul(cur.ap[:, a4 * 128:(a4 + 1) * 128], vb.ap[:, tau, j2, :], pt.ap[:, 0:128], start=(not has_prev), stop=True))(cur, pt, tau, a4, has_prev, vb, j2)]
                            wr = [cur]
                            if has_next:
                                if a4 == 3:
                                    ost["nxt"] = oring.next()
                                    tgt, b4 = ost["nxt"], 0
                                else:
                                    tgt, b4 = cur, a4 + 1
                                fns.append((lambda tgt, pt, tau, b4, vb, j2: lambda e: e.matmul(tgt.ap[:, b4 * 128:(b4 + 1) * 128], vb.ap[:, tau, j2, :], pt.ap[:, 128:256], start=True, stop=False))(tgt, pt, tau, b4, vb, j2))
                                if tgt is not cur:
                                    wr.append(tgt)
                            S.group("tensor", fns, reads=[pt, vb], writes=wr)
                            if a4 == 3:
                                m4 = tau // 4
                                if g == 0:
                                    od = oT[ro, ch, m4 * 512:(m4 + 1) * 512]
                                    ld = LS[rl, jj, m4 * 512:(m4 + 1) * 512]
                                    oi = cur.ap[ro, :]
                                    li = cur.ap[rl, :]
                                elif g == 1:
                                    od = oT[ro, ch, m4:T:4]
                                    ld = LS[rl, jj, m4:T:4]
                                    oi = cur.ap[ro, :]
                                    li = cur.ap[rl, :]
                                else:
                                    od = oT[ro, ch, :].rearrange("p (l r) -> p r l", r=16)[:, m4 * 4:(m4 + 1) * 4, :]
                                    ld = LS[rl, jj, :].rearrange("p (l r) -> p r l", r=16)[:, m4 * 4:(m4 + 1) * 4, :]
                                    oi = cur.ap[ro, :].rearrange("p (a l) -> p a l", a=4)
                                    li = cur.ap[rl, :].rearrange("p (a l) -> p a l", a=4)
                                S.op("scalar", (lambda od, oi: lambda e: e.copy(od, oi))(od, oi), reads=[cur], writes=[o_tb])
                                if g == 0:
                                    S.op("vector", (lambda ld, li: lambda e: e.tensor_copy(ld, li))(ld, li), reads=[cur], writes=[ls_tb])
                                else:
                                    S.op("vector", (lambda ld, li: lambda e: e.tensor_tensor(ld, ld, li, ALU.add))(ld, li), reads=[cur], writes=[ls_tb])

                        ns = len(steps)
                        for si in range(ns + LAG):
                            if si < ns:
                                stage_s(si)
                            if si - LAG >= 0:
                                stage_p(si - LAG)
                            if pgen is not None and si % 5 in (1, 3):
                                next(pgen, None)
                        if pgen is not None:
                            for _ in pgen:
                                pass

                    wcur = load_unit(0)
                    for _ in proj(0, wcur):
                        pass
                    wn = load_unit(1)
                    for ui in range(len(units)):
                        if ui + 1 < len(units):
                            attn(ui, proj(ui + 1, wn))
                            if ui + 2 < len(units):
                                wn = load_unit(ui + 2)
                        else:
                            attn(ui)
                    S.flush()
                S.barrier()
                with ExitStack() as os_:
                    recr = Ring([sb(os_, "a_rec%d" % i, [128, 2, 512], F32) for i in range(2)])
                    yring = Ring([sb(os_, "a_y%d" % i, [128, 8, 512], F32) for i in range(2)])
                    rsd = Resid(os_, "ar", l, 1, hsrc, hdst, depth=2, depth_sq=1, nxt=(l, 2, xn_next[0], xn_next[1]))
                    pring = Ring([None] * 4)
                    pring.bufs = [psb[i] for i in range(4)]
                    ybs = {}

                    def mm_stage(tb):
                        tsl = slice(tb * 512, (tb + 1) * 512)
                        rc = recr.next()
                        S.op("scalar", (lambda tsl, rc: lambda e: e.activation(rc.ap[0:64, :, :], LS[64:128, :, tsl], AF.Ln))(tsl, rc), reads=[ls_tb], writes=[rc])
                        S.op("scalar", (lambda tsl, rc: lambda e: e.activation(rc.ap[64:128, :, :], LS[0:64, :, tsl], AF.Ln))(tsl, rc), reads=[ls_tb], writes=[rc])
                        S.op("scalar", (lambda rc: lambda e: e.activation(rc.ap[:, :, :], rc.ap[:, :, :], AF.Exp, scale=-1.0))(rc), reads=[rc], writes=[rc])
                        for g in range(3):
                            S.op("gpsimd", (lambda g, tsl, rc: lambda e: e.tensor_tensor(oT[:, 2 * g:2 * g + 2, tsl], oT[:, 2 * g:2 * g + 2, tsl], rc.ap[:, :, :], ALU.mult))(g, tsl, rc), reads=[rc], writes=[o_tb])
                        yb = yring.next()
                        ybs[tb] = yb
                        for oc in range(8):
                            p = pring.next()
                            S.group("tensor", [(lambda k, p, oc, tsl: lambda e: e.matmul(p.ap[:, :], wo[:, k, oc * 128:(oc + 1) * 128], oT[:, k, tsl], start=(k == 0), stop=(k == 5)))(k, p, oc, tsl) for k in range(6)],
                                    reads=[WO, o_tb], writes=[p])
                            if oc % 2 == 0:
                                S.op("scalar", (lambda p, oc, yb: lambda e: e.copy(yb.ap[:, oc, :], p.ap[:, :]))(p, oc, yb), reads=[p], writes=[yb])
                            else:
                                S.op("vector", (lambda p, oc, yb: lambda e: e.tensor_copy(yb.ap[:, oc, :], p.ap[:, :]))(p, oc, yb), reads=[p], writes=[yb])

                    def ra(tb):
                        rsd.step_a(tb, ybs[tb].ap[:, :, :], ybs[tb])
                    mm_stage(0); mm_stage(1); ra(0); mm_stage(2); rsd.step_b(0); ra(1); mm_stage(3); rsd.step_b(1); ra(2); rsd.step_b(2); ra(3); rsd.step_b(3)
                    S.flush()
                S.barrier()

        def kvf_phase(hsrc):
            with ExitStack() as fs:
                xnT = sb(fs, "k_xn", [128, 8, T], BF16)
                xn_tbs = [TB() for _ in range(NTB)]
                with ExitStack() as ns:
                    norm_phase(ns, hsrc, DEPTH, 0, xn_tbs, xnT)
                    S.flush()
                S.barrier()
                with ExitStack() as ks:
                    wl = WLoader(ks, "wk", [8, 128], 2, 3, eng=("scalar", "vector"))
                    kst = Ring([sb(ks, "k_st%d" % i, [64, T], BF16) for i in range(4)])
                    vst = Ring([sb(ks, "k_vs%d" % i, [128, 16, 128], BF16) for i in range(2)])
                    pring = Ring([None] * 4)
                    pring.bufs = [psb[i] for i in range(4)]
                    for pc in range(8):
                        w = wl.load(wk_d[pc])
                        ks0 = kst.next()
                        i0 = kst.i
                        ks1 = kst.next()
                        i1 = kst.i
                        for tb in range(NTB):
                            p = pring.next()
                            tsl = slice(tb * 512, (tb + 1) * 512)
                            S.group("tensor", [(lambda k, p, w, tsl: lambda e: e.matmul(p.ap[:, :], w.ap[:, k, :], xnT[:, k, tsl], start=(k == 0), stop=(k == 7)))(k, p, w, tsl) for k in range(8)],
                                    reads=[w, xn_tbs[tb]], writes=[p])
                            S.op("scalar", (lambda ks0, p, tsl: lambda e: e.copy(ks0.ap[:, tsl], p.ap[0:64, :]))(ks0, p, tsl), reads=[p], writes=[ks0])
                            S.op("scalar", (lambda ks1, p, tsl: lambda e: e.copy(ks1.ap[:, tsl], p.ap[64:128, :]))(ks1, p, tsl), reads=[p], writes=[ks1])
                        S.dma("kst%d" % i0, (lambda ks_, h: lambda e: e.dma_start(out=Kd[h], in_=ks_.ap[:, :]))(ks0, 2 * pc), reads=[ks0], eng="gpsimd")
                        S.dma("kst%d" % i1, (lambda ks_, h: lambda e: e.dma_start(out=Kd[h], in_=ks_.ap[:, :]))(ks1, 2 * pc + 1), reads=[ks1], eng="gpsimd")
                        w = wl.load(wv_d[pc])
                        vs_ = vst.next()
                        for m4 in range(4):
                            p = pring.next()
                            fns = []
                            for a4 in range(4):
                                tau = m4 * 4 + a4
                                for k in range(8):
                                    fns.append((lambda k, p, w, a4, tau: lambda e: e.matmul(p.ap[:, a4 * 128:(a4 + 1) * 128], xnT[:, k, tau * 128:(tau + 1) * 128], w.ap[:, k, :], start=(k == 0), stop=(k == 7)))(k, p, w, a4, tau))
                            S.group("tensor", fns, reads=[w] + xn_tbs, writes=[p])
                            S.op("scalar", (lambda vs_, p, m4: lambda e: e.copy(vs_.ap[:, m4 * 4:(m4 + 1) * 4, :], p.ap[:, :].rearrange("p (a d) -> p a d", a=4)))(vs_, p, m4), reads=[p], writes=[vs_])
                        for j2 in range(2):
                            S.dma("vst%d_%d" % (vst.i, j2), (lambda vs_, h, j2: lambda e: e.dma_start(out=Vd[h], in_=vs_.ap[:, :, 64 * j2:64 * j2 + 64]))(vs_, 2 * pc + j2, j2), reads=[vs_], eng="gpsimd")
                    wfs = sb(ks, "k_wfs", [128, 8, 16], F32)
                    wfb = sb(ks, "k_wfb", [128, 8, 16], BF16)
                    WF = TB()
                    S.dma("wf", lambda e: e.dma_start(out=wfs[:, :, :], in_=wf_d[:, :, :]), writes=[WF])
                    S.op("gpsimd", lambda e: e.tensor_copy(wfb[:, :, :], wfs[:, :, :]), reads=[WF], writes=[WF])
                    lf = sb(ks, "k_lf", [16, T], F32)
                    cT = sb(ks, "k_cT", [16, T], F32)
                    r1 = sb(ks, "k_r1", [16, T], F32)
                    onesf = sb(ks, "k_on", [16, T], F32)
                    cq3 = sb(ks, "k_cq", [16, 3, T], BF16)
                    LF = TB()
                    S.op("gpsimd", lambda e: e.memset(onesf[:, :], 1.0), writes=[LF])
                    for tb in range(NTB):
                        p = pring.next()
                        tsl = slice(tb * 512, (tb + 1) * 512)
                        S.group("tensor", [(lambda k, p, tsl: lambda e: e.matmul(p.ap[0:16, :], wfb[:, k, :], xnT[:, k, tsl], start=(k == 0), stop=(k == 7)))(k, p, tsl) for k in range(8)],
                                reads=[WF, xn_tbs[tb]], writes=[p])
                        S.op("scalar", (lambda p, tsl: lambda e: e.activation(lf[:, tsl], p.ap[0:16, :], AF.Exp, bias=bfneg[:, 0:1], scale=-1.0))(p, tsl), reads=[p, CONST], writes=[LF])
                        S.op("scalar", (lambda tsl: lambda e: e.activation(lf[:, tsl], lf[:, tsl], AF.Ln, bias=onec[0:16, 0:1], scale=1.0))(tsl), reads=[LF], writes=[LF])
                    S.op("vector", lambda e: e.tensor_tensor_scan(cT[:, :], onesf[:, :], lf[:, :], 0.0, ALU.mult, ALU.subtract), reads=[LF], writes=[LF])
                    S.op("vector", lambda e: e.tensor_scalar(r1[:, :], cT[:, :], 8.0, None, ALU.mult), reads=[LF], writes=[LF])
                    S.op("vector", lambda e: e.tensor_copy(cq3[:, 0, :], r1[:, :]), reads=[LF], writes=[LF])
                    S.op("vector", lambda e: e.tensor_tensor(r1[:, :], r1[:, :], cq3[:, 0, :], ALU.subtract), reads=[LF], writes=[LF])
                    S.op("vector", lambda e: e.tensor_copy(cq3[:, 1, :], r1[:, :]), reads=[LF], writes=[LF])
                    S.op("vector", lambda e: e.tensor_tensor(r1[:, :], r1[:, :], cq3[:, 1, :], ALU.subtract), reads=[LF], writes=[LF])
                    S.op("vector", lambda e: e.tensor_copy(cq3[:, 2, :], r1[:, :]), reads=[LF], writes=[LF])
                    S.dma("cq", lambda e: e.dma_start(out=Cq[:, :, :], in_=cq3[:, :, :]), reads=[LF])
                    p = pring.next()
                    S.group("tensor", [(lambda tau, p: lambda e: e.transpose(p.ap[:, tau * 16:(tau + 1) * 16], cT[:, tau * 128:(tau + 1) * 128], ident[:, :]))(tau, p) for tau in range(16)],
                            reads=[LF, CONST], writes=[p])
                    S.op("scalar", (lambda p: lambda e: e.mul(ckneg[:, :], p.ap[:, 0:256], -1.0))(p), reads=[p], writes=[CKNEG])
                    S.flush()
                S.barrier()

        def layer_b(l, hsrc, hdst, xn_next):
            j = l - NA
            with ExitStack() as fs:
                oT = sb(fs, "b_o", [128, 8, T], BF16)
                wo = sb(fs, "b_wo", [128, 8, 1024], BF16)
                WO = TB()
                o_tb = TB()
                with ExitStack() as as_:
                    xnT = xn_next[0]
                    xn_tbs = [TB() for _ in range(NTB)]
                    with ExitStack() as ns:
                        norm_phase(ns, hsrc, l, 0, xn_tbs, xnT)
                        S.flush()
                    S.barrier()
                    wsr = Ring([sb(as_, "b_wos%d" % i, [128, 1024], F32) for i in range(4)])
                    for kc in range(8):
                        wst = wsr.next()
                        S.dma("wo%d" % wsr.i, (lambda kc, wst: lambda e: e.dma_start(out=wst.ap[:, :], in_=wob_d[j, :, kc, :]))(kc, wst), writes=[wst])
                        ce = ("gpsimd", "vector", "scalar")[kc % 3]
                        if ce == "scalar":
                            S.op("scalar", (lambda kc, wst: lambda e: e.copy(wo[:, kc, :], wst.ap[:, :]))(kc, wst), reads=[wst], writes=[WO])
                        else:
                            S.op(ce, (lambda kc, wst: lambda e: e.tensor_copy(wo[:, kc, :], wst.ap[:, :]))(kc, wst), reads=[wst], writes=[WO])
                    wl = WLoader(as_, "wq", [8, 128], 2, 2, eng=("vector", "gpsimd"))
                    qa = Ring([sb(as_, "b_q%d" % i, [67, T], BF16) for i in range(4)])
                    ka = Ring([sb(as_, "b_k%d" % i, [67, T], BF16) for i in range(4)])
                    va = [Ring([sb(as_, "b_v%d_%d" % (par, i), [128, 16, 128], BF16) for i in range(2)]) for par in range(2)]
                    for b in ka.bufs:
                        S.op("gpsimd", (lambda b: lambda e: e.memset(b.ap[64:67, :], 1.0))(b), writes=[b])
                    for par in range(2):
                        for b in va[par].bufs:
                            S.op("gpsimd", (lambda b: lambda e: e.memset(b.ap[:, :, :], 1.0))(b), writes=[b])
                    ptr = Ring([sb(as_, "b_pt%d" % i, [128, 512], BF16) for i in range(8)])
                    recr = Ring([sb(as_, "b_rc%d" % i, [128, 512], F32) for i in range(2)])
                    pring = Ring([None] * 2)
                    pring.bufs = [psb[0], psb[1]]
                    sring = Ring([None] * 4)
                    sring.bufs = [psb[2], psb[3], psb[4], psb[7]]
                    oring = Ring([None] * 2)
                    oring.bufs = [psb[5], psb[6]]
                    LAG = 3
                    hd = {}
                    wq = {}

                    def pair_loads(pc):
                        wq[pc] = wl.load(wqb_d[j, pc])
                        for par in range(2):
                            h = 2 * pc + par
                            q = qa.next()
                            kk = ka.next()
                            vv = va[par].next()
                            S.dma("bk%d" % ka.i, (lambda kk, h: lambda e: e.dma_start(out=kk.ap[0:64, :], in_=Kd[h]))(kk, h), writes=[kk])
                            S.dma("bv%d_%d" % (par, va[par].i), (lambda vv, h, par: lambda e: e.dma_start(out=vv.ap[:, :, 64 * par:64 * par + 64], in_=Vd[h]))(vv, h, par), writes=[vv])
                            S.dma("bq%d" % qa.i, (lambda q, h: lambda e: e.dma_start(out=q.ap[64:67, :], in_=Cq[h]))(q, h), writes=[q])
                            hd[h] = (q, kk, vv, par)

                    def pair_proj(pc, tb):
                        w = wq[pc]
                        q0_ = hd[2 * pc][0]
                        q1_ = hd[2 * pc + 1][0]
                        p = pring.next()
                        tsl = slice(tb * 512, (tb + 1) * 512)
                        S.group("tensor", [(lambda k: lambda e: e.matmul(p.ap[:, :], w.ap[:, k, :], xnT[:, k, tsl], start=(k == 0), stop=(k == 7)))(k) for k in range(8)],
                                reads=[w, xn_tbs[tb]], writes=[p])
                        S.op("vector", lambda e: e.tensor_copy(q0_.ap[0:64, tsl], p.ap[0:64, :]), reads=[p], writes=[q0_])
                        S.op("vector", lambda e: e.tensor_copy(q1_.ap[0:64, tsl], p.ap[64:128, :]), reads=[p], writes=[q1_])

                    steps = [(h, G, tau) for h in range(16) for G in range(4) for tau in range(4 * G + 4)]
                    pts = {}
                    obs = {}
                    PROJ_AT = {(2, 0): 0, (2, 4): 1, (3, 0): 2, (3, 6): 3}

                    def stage_s(si):
                        h, G, tau = steps[si]
                        if h % 2 == 1 and h + 1 < 16:
                            pc = (h + 1) // 2
                            if G == 1 and tau == 0:
                                pair_loads(pc)
                            if (G, tau) in PROJ_AT:
                                pair_proj(pc, PROJ_AT[(G, tau)])
                        q, kk, vv, par = hd[h]
                        off = max(0, tau - 4 * G) * 128
                        n = 512 - off
                        q0 = 512 * G + off
                        sp = sring.next()
                        diag = tau >= 4 * G
                        fns_ = [lambda e: e.matmul(sp.ap[:, 0:n], kk.ap[0:67, tau * 128:(tau + 1) * 128], q.ap[0:67, q0:q0 + n], start=True, stop=not diag)]
                        if diag:
                            fns_.append(lambda e: e.matmul(sp.ap[:, 0:128], identb[:, :], maskB[:, 0:128], start=False, stop=True))
                        S.group("tensor", fns_, reads=[kk, q, CONST], writes=[sp])
                        pt = ptr.next()
                        S.op("scalar", lambda e: e.activation(pt.ap[:, 0:n], sp.ap[:, 0:n], AF.Exp, bias=ckneg[:, tau * 16 + h:tau * 16 + h + 1], scale=0.125),
                             reads=[sp, CKNEG], writes=[pt])
                        pts[si] = pt

                    def stage_p(si):
                        h, G, tau = steps[si]
                        q, kk, vv, par = hd[h]
                        ro = slice(64 * par, 64 * par + 64)
                        rl = slice(64 - 64 * par, 128 - 64 * par)
                        off = max(0, tau - 4 * G) * 128
                        n = 512 - off
                        last = 4 * G + 3
                        pt = pts.pop(si)
                        if tau == 0:
                            obs[(h, G)] = oring.next()
                        ob = obs[(h, G)]
                        S.op("tensor", lambda e: e.matmul(ob.ap[:, off:512], vv.ap[:, tau, :], pt.ap[:, 0:n], start=(tau == 0), stop=(tau == last)),
                             reads=[pt, vv], writes=[ob])
                        if tau == last:
                            rc = recr.next()
                            S.op("vector", lambda e: e.reciprocal(rc.ap[ro, :], ob.ap[rl, :]), reads=[ob], writes=[rc])
                            S.op("vector", lambda e: e.tensor_tensor(oT[ro, h // 2, G * 512:(G + 1) * 512], ob.ap[ro, :], rc.ap[ro, :], ALU.mult), reads=[ob, rc], writes=[o_tb])

                    pair_loads(0)
                    for tb in range(NTB):
                        pair_proj(0, tb)
                    ns_ = len(steps)
                    for si in range(ns_ + LAG):
                        if si < ns_:
                            stage_s(si)
                        if si - LAG >= 0:
                            stage_p(si - LAG)
                    S.flush()
                S.barrier()
                with ExitStack() as os_:
                    yring = Ring([sb(os_, "b_y%d" % i, [128, 8, 512], F32) for i in range(2)])
                    rsd = Resid(os_, "br", l, 1, hsrc, hdst, depth=2, depth_sq=1, nxt=(l, 2, xn_next[0], xn_next[1]))
                    pring = Ring([None] * 4)
                    pring.bufs = [psb[i] for i in range(4)]
                    ybs = {}

                    def mm_stage(tb):
                        tsl = slice(tb * 512, (tb + 1) * 512)
                        yb = yring.next()
                        ybs[tb] = yb
                        for oc in range(8):
                            p = pring.next()
                            S.group("tensor", [(lambda k, p, oc, tsl: lambda e: e.matmul(p.ap[:, :], wo[:, k, oc * 128:(oc + 1) * 128], oT[:, k, tsl], start=(k == 0), stop=(k == 7)))(k, p, oc, tsl) for k in range(8)],
                                    reads=[WO, o_tb], writes=[p])
                            if oc % 2 == 0:
                                S.op("scalar", (lambda p, oc, yb: lambda e: e.copy(yb.ap[:, oc, :], p.ap[:, :]))(p, oc, yb), reads=[p], writes=[yb])
                            else:
                                S.op("vector", (lambda p, oc, yb: lambda e: e.tensor_copy(yb.ap[:, oc, :], p.ap[:, :]))(p, oc, yb), reads=[p], writes=[yb])

                    def ra(tb):
                        rsd.step_a(tb, ybs[tb].ap[:, :, :], ybs[tb])
                    mm_stage(0); mm_stage(1); ra(0); mm_stage(2); rsd.step_b(0); ra(1); mm_stage(3); rsd.step_b(1); ra(2); rsd.step_b(2); ra(3); rsd.step_b(3)
                    S.flush()
                S.barrier()

        stages = []
        for l in range(DEPTH):
            if l == NA:
                stages.append(("kvf", l))
            stages.append(("mix", l))
            stages.append(("ffn", l))
        if stop is not None:
            stages = stages[:stop]
        n_h = sum(1 for s in stages if s[0] != "kvf")
        cur = xT
        hi = 0
        pp = [hA, hB]
        with ExitStack() as ls_:
            xn_f = None
            for (kind, l) in stages:
                if kind == "kvf":
                    kvf_phase(cur)
                    continue
                hi += 1
                dst = outT if hi == n_h else pp[hi % 2]
                if kind == "mix":
                    ls_.close()
                    xn_f = (sb(ls_, "xn_f", [128, 8, T], BF16), [TB() for _ in range(NTB)])
                    if l < NA:
                        layer_a(l, cur, dst, xn_f)
                    else:
                        layer_b(l, cur, dst, xn_f)
                else:
                    ffn(l, cur, dst, xn_f)
                cur = dst
        S.flush(final=True)
    return nc


def _tile_w(w, kc):
    n = w.shape[1]
    return np.ascontiguousarray(w.reshape(kc, 128, n).transpose(1, 0, 2))


def prep_shared(inputs):
    f = lambda a: np.asarray(a, dtype=np.float32)
    g = f(inputs["norm_gains"])
    kvn = f(inputs["kv_norm"])
    gc = np.zeros((128, 136), np.float32)
    for l in range(DEPTH):
        for i in range(4):
            gc[:, (l * 4 + i) * 8:(l * 4 + i) * 8 + 8] = g[l, i].reshape(8, 128).T
    gc[:, 128:136] = kvn.reshape(8, 128).T
    wqkv = f(inputs["w_qkv_a"])
    perm = np.arange(64)
    perm[0:8] = np.arange(8, 16)
    perm[8:16] = np.arange(0, 8)
    wq_t = np.zeros((NA, 6, 5, 128, 8, 128), np.float32)
    for l in range(NA):
        for gg in range(3):
            for jj in range(2):
                u = gg * 2 + jj
                c0 = (gg * 4 + jj * 2) * 64
                for pi, base in ((0, 0), (2, 768), (4, 1536)):
                    blk = wqkv[l][:, base + c0:base + c0 + 128]
                    wq_t[l, u, pi] = _tile_w(blk, 8)
                    if pi < 4:
                        sw = blk.reshape(1024, 2, 64)[:, :, perm].reshape(1024, 128)
                        wq_t[l, u, pi + 1] = _tile_w(sw, 8)
    woa = np.stack([_tile_w(f(inputs["w_o_a"])[l], 6) for l in range(NA)])
    wqb_full = f(inputs["w_q_b"])
    wqb = np.stack([np.stack([_tile_w(wqb_full[j][:, pc * 128:(pc + 1) * 128], 8) for pc in range(8)]) for j in range(2)])
    wob = np.stack([_tile_w(f(inputs["w_o_b"])[j], 8) for j in range(2)])
    wkvf = f(inputs["w_kvf"])
    wk = np.stack([_tile_w(wkvf[:, pc * 128:(pc + 1) * 128], 8) for pc in range(8)])
    wv = np.stack([_tile_w(wkvf[:, 1024 + pc * 128:1024 + (pc + 1) * 128], 8) for pc in range(8)])
    wf = _tile_w(wkvf[:, 2048:2064], 8)
    wup_full = f(inputs["w_up"])
    wup = np.zeros((DEPTH, NFC, 128, 8, 256), np.float32)
    for l in range(DEPTH):
        for fc in range(NFC):
            wup[l, fc, :, :, 0:128] = _tile_w(wup_full[l][:, fc * 128:(fc + 1) * 128], 8)
            wup[l, fc, :, :, 128:256] = _tile_w(wup_full[l][:, DFF + fc * 128:DFF + (fc + 1) * 128], 8)
    cw = f(inputs["conv_w"])
    cb = f(inputs["conv_b"])
    cpm = np.zeros((128, DEPTH, 44, 4), np.float32)
    for l in range(DEPTH):
        for k in range(3):
            cpm[:, l, :, k] = cw[l, k].reshape(44, 128).T
        cpm[:, l, :, 3] = cb[l].reshape(44, 128).T
    wdn_full = f(inputs["w_down"])
    wdn = np.stack([np.stack([_tile_w(wdn_full[l][:, oc * 128:(oc + 1) * 128], NFC) for oc in range(8)]) for l in range(DEPTH)])
    pos = np.arange(T, dtype=np.float32)
    inv = (np.float32(500000.0) ** (-(np.arange(0, 16, 2, dtype=np.float32)) / np.float32(16))).astype(np.float32)
    ang = (pos[:, None] * inv[None, :]).astype(np.float32)
    cos = np.cos(ang.astype(np.float64)).astype(np.float32).T
    sin = np.sin(ang.astype(np.float64)).astype(np.float32).T
    rc = np.ones((128, T), np.float32)
    rs = np.zeros((128, T), np.float32)
    for hb in (0, 64):
        rc[hb:hb + 8] = cos
        rc[hb + 8:hb + 16] = cos
        rs[hb:hb + 8] = -sin
        rs[hb + 8:hb + 16] = sin
    permR = np.zeros((128, 128), np.float32)
    for m in range(128):
        mh = m % 64
        if mh < 8:
            permR[m + 8, m] = 1.0
        elif mh < 16:
            permR[m - 8, m] = 1.0
    kk = np.arange(128)[:, None]
    qq = np.arange(128)[None, :]
    maskA = np.concatenate([(qq >= kk), (kk >= qq)], axis=1).astype(np.float32)
    return {
        "gcols": gc, "wqkv": wq_t, "woa": woa, "wqb": wqb, "wob": wob, "wk": wk, "wv": wv, "wf": wf,
        "bfneg": np.ascontiguousarray(-f(inputs["b_f"]).reshape(16, 1)),
        "wup": wup, "cp": np.ascontiguousarray(cpm.reshape(128, DEPTH * 44 * 4)), "wdn": wdn,
        "ropeC": rc, "ropeS": rs, "maskA": maskA, "ident": np.eye(16, dtype=np.float32),
        "ident128": np.eye(128, dtype=np.float32), "permR": permR,
    }


_NC_CACHE = {}


def kernel(**inputs):
    x = np.asarray(inputs["x"], dtype=np.float32)
    shared = prep_shared(inputs)
    if "nc" not in _NC_CACHE:
        _NC_CACHE["nc"] = build()
    nc = _NC_CACHE["nc"]
    in_maps = []
    for b in range(8):
        m = dict(shared)
        m["xT"] = np.ascontiguousarray(x[b].T)
        in_maps.append(m)
    res = run_bass_kernel_spmd(nc, in_maps, core_ids=list(range(8)))
    out = np.stack([np.ascontiguousarray(res.results[b]["outT"].T) for b in range(8)])
    return out.astype(np.float32)
```
